# Optimizing a Trainium2 kernel written in Bass

```python
import math
import jax
import jax.numpy as jnp
from jax import lax
import numpy as np

D_MODEL = 1024
BATCH = 2
SEQ = 8192
DEPTH = 1

HEAD_DIM = 64
A_Q_HEADS = 8
A_KV_HEADS = 2
B_Q_HEADS = 8
B_KV_HEADS = 2
BRANCH_WIDTH = A_Q_HEADS * HEAD_DIM
D_FF = 4 * D_MODEL
GRID_W = 64
Q_BLOCK = 128
WINDOW = 128
BAND_BLOCK = WINDOW
ROPE_THETA = 10000.0
AXIAL_THETA = 10000.0
NORM_EPS = 1e-6
NEG_INF = -1e30

A_Q_W = A_Q_HEADS * HEAD_DIM
A_KV_W = A_KV_HEADS * HEAD_DIM
B_Q_W = B_Q_HEADS * HEAD_DIM
B_KV_W = B_KV_HEADS * HEAD_DIM
IN_SPLITS = [A_Q_W, A_KV_W, A_KV_W, B_Q_W, B_KV_W, B_KV_W, D_MODEL, D_MODEL]
IN_WIDTH = sum(IN_SPLITS)

kernel_name = "hybrid_gated_axial_window_attention_block"


def rms_norm(x, g):
    xf = x.astype(jnp.float32)
    y = xf * lax.rsqrt(jnp.mean(xf * xf, axis=-1, keepdims=True) + NORM_EPS)
    return (y * g.astype(jnp.float32)).astype(x.dtype)


def rope_cos_sin(pos, dim, theta):
    inv = theta ** (-jnp.arange(0, dim, 2, dtype=jnp.float32) / dim)
    ang = pos.astype(jnp.float32)[:, None] * inv[None, :]
    return jnp.cos(ang), jnp.sin(ang)


def apply_rope(x, cos, sin):
    xf = x.astype(jnp.float32)
    half = xf.shape[-1] // 2
    x1, x2 = xf[..., :half], xf[..., half:]
    c = cos[None, :, None, :]
    s = sin[None, :, None, :]
    return jnp.concatenate([x1 * c - x2 * s, x1 * s + x2 * c], axis=-1).astype(x.dtype)


def apply_axial_rope(x, row, col):
    half = x.shape[-1] // 2
    cr, sr = rope_cos_sin(row, half, AXIAL_THETA)
    cc, sc = rope_cos_sin(col, half, AXIAL_THETA)
    return jnp.concatenate([apply_rope(x[..., :half], cr, sr),
                            apply_rope(x[..., half:], cc, sc)], axis=-1)


def global_attention(q, k, v):
    b, s, hq, dh = q.shape
    hkv = k.shape[2]
    g = hq // hkv
    nb = s // Q_BLOCK
    scale = dh ** -0.5
    qb = q.reshape(b, nb, Q_BLOCK, hkv, g, dh).transpose(1, 0, 2, 3, 4, 5)

    def one_block(qblk):
        sc = jnp.einsum('bqkgd,bskd->bkgqs', qblk, k).astype(jnp.float32) * scale
        p = jax.nn.softmax(sc, axis=-1).astype(v.dtype)
        return jnp.einsum('bkgqs,bskd->bqkgd', p, v)

    o = lax.map(one_block, qb)
    return o.transpose(1, 0, 2, 3, 4, 5).reshape(b, s, hq * dh)


def window_sink_attention(q, k, v, sink):
    b, s, hq, dh = q.shape
    hkv = k.shape[2]
    g = hq // hkv
    nb = s // BAND_BLOCK
    scale = dh ** -0.5
    pad = ((0, 0), (BAND_BLOCK, BAND_BLOCK), (0, 0), (0, 0))
    kr = jnp.pad(k, pad).reshape(b, nb + 2, BAND_BLOCK, hkv, dh)
    vr = jnp.pad(v, pad).reshape(b, nb + 2, BAND_BLOCK, hkv, dh)
    kb = jnp.concatenate([kr[:, :-2], kr[:, 1:-1], kr[:, 2:]], axis=2)
    vb = jnp.concatenate([vr[:, :-2], vr[:, 1:-1], vr[:, 2:]], axis=2)
    qb = q.reshape(b, nb, BAND_BLOCK, hkv, g, dh)
    sc = jnp.einsum('bnqkgd,bnskd->bnkgqs', qb, kb).astype(jnp.float32) * scale
    blk = jnp.arange(nb, dtype=jnp.int32)[:, None] * BAND_BLOCK
    qpos = blk + jnp.arange(BAND_BLOCK, dtype=jnp.int32)[None, :]
    kpos = blk - BAND_BLOCK + jnp.arange(3 * BAND_BLOCK, dtype=jnp.int32)[None, :]
    valid = (jnp.abs(kpos[:, None, :] - qpos[:, :, None]) <= WINDOW) \
        & (kpos[:, None, :] >= 0) & (kpos[:, None, :] < s)
    sc = jnp.where(valid[None, :, None, None], sc, NEG_INF)
    sink_l = jnp.broadcast_to(sink.astype(jnp.float32).reshape(1, 1, hkv, g, 1, 1),
                              sc.shape[:-1] + (1,))
    p = jax.nn.softmax(jnp.concatenate([sc, sink_l], axis=-1), axis=-1)[..., :-1]
    o = jnp.einsum('bnkgqs,bnskd->bnqkgd', p.astype(v.dtype), vb)
    return o.reshape(b, s, hq * dh)


def setup_inputs(seed: int = 0) -> dict:
    key = jax.random.key(seed)
    ks = jax.random.split(key, 16)
    f32 = jnp.float32
    d = D_MODEL

    def nrm(k, shape, scale):
        return jax.random.normal(k, shape, f32) * scale

    return {
        "x": nrm(ks[0], (BATCH, SEQ, d), 1.0),
        "c": nrm(ks[1], (BATCH, d), 1.0),
        "w_ada": nrm(ks[2], (DEPTH, d, 6 * d), 0.02),
        "b_ada": nrm(ks[3], (DEPTH, 6 * d), 0.02),
        "norm1_g": 1.0 + nrm(ks[4], (DEPTH, d), 0.02),
        "w_in": nrm(ks[5], (DEPTH, d, IN_WIDTH), d ** -0.5),
        "q_norm_a": 1.0 + nrm(ks[6], (DEPTH, HEAD_DIM), 0.02),
        "k_norm_a": 1.0 + nrm(ks[7], (DEPTH, HEAD_DIM), 0.02),
        "sink_b": nrm(ks[8], (DEPTH, B_Q_HEADS), 0.5),
        "w_branch": nrm(ks[9], (DEPTH, 2, BRANCH_WIDTH, d), BRANCH_WIDTH ** -0.5),
        "w_out": nrm(ks[10], (DEPTH, d, d), d ** -0.5),
        "norm2_g": 1.0 + nrm(ks[11], (DEPTH, d), 0.02),
        "w_mlp_in": nrm(ks[12], (DEPTH, d, D_FF), d ** -0.5),
        "w_mlp_out": nrm(ks[13], (DEPTH, D_FF, d), D_FF ** -0.5),
        "final_g": 1.0 + nrm(ks[14], (d,), 0.02),
    }


def reference(x, c, w_ada, b_ada, norm1_g, w_in, q_norm_a, k_norm_a, sink_b,
              w_branch, w_out, norm2_g, w_mlp_in, w_mlp_out, final_g):
    b, s, d = x.shape
    rows = s // GRID_W
    t = jnp.arange(s, dtype=jnp.int32)
    row_ids = jnp.repeat(jnp.arange(rows, dtype=jnp.int32), GRID_W)
    col_ids = jnp.tile(jnp.arange(GRID_W, dtype=jnp.int32), rows)
    cos1, sin1 = rope_cos_sin(t, HEAD_DIM, ROPE_THETA)
    offsets = np.cumsum(IN_SPLITS)[:-1].tolist()

    for l in range(DEPTH):
        mod = jax.nn.silu(c) @ w_ada[l] + b_ada[l]
        shift1, scale1, gate1, shift2, scale2, gate2 = jnp.split(mod, 6, axis=-1)

        h = rms_norm(x, norm1_g[l]) * (1.0 + scale1[:, None]) + shift1[:, None]
        proj = h @ w_in[l]
        qa, ka, va, qb, kb, vb, ga, gb = jnp.split(proj, offsets, axis=-1)

        qa = rms_norm(qa.reshape(b, s, A_Q_HEADS, HEAD_DIM), q_norm_a[l])
        ka = rms_norm(ka.reshape(b, s, A_KV_HEADS, HEAD_DIM), k_norm_a[l])
        qa = apply_axial_rope(qa, row_ids, col_ids)
        ka = apply_axial_rope(ka, row_ids, col_ids)
        ya = global_attention(qa, ka, va.reshape(b, s, A_KV_HEADS, HEAD_DIM))

        qb = apply_rope(qb.reshape(b, s, B_Q_HEADS, HEAD_DIM), cos1, sin1)
        kb = apply_rope(kb.reshape(b, s, B_KV_HEADS, HEAD_DIM), cos1, sin1)
        yb = window_sink_attention(qb, kb, vb.reshape(b, s, B_KV_HEADS, HEAD_DIM), sink_b[l])

        ua = ya @ w_branch[l, 0]
        ub = yb @ w_branch[l, 1]
        merged = jax.nn.sigmoid(ga) * ua + jax.nn.sigmoid(gb) * ub
        x = x + gate1[:, None] * (merged @ w_out[l])

        h2 = rms_norm(x, norm2_g[l]) * (1.0 + scale2[:, None]) + shift2[:, None]
        hid = jnp.square(jax.nn.relu(h2 @ w_mlp_in[l]))
        x = x + gate2[:, None] * (hid @ w_mlp_out[l])

    return rms_norm(x, final_g)
```

```python
from contextlib import ExitStack
import os
import numpy as np
import concourse.bass as bass
import concourse.mybir as mybir
from concourse.bass_utils import run_bass_kernel_spmd

F32 = mybir.dt.float32
BF16 = mybir.dt.bfloat16
U8 = mybir.dt.uint8
ALU = mybir.AluOpType
AF = mybir.ActivationFunctionType
AX = mybir.AxisListType

N_CORES = 8
D = 1024
SEQ = 8192
NOWN = 16
NOTH = 48
EPS = 1e-6
NEG = -30000.0


class _Op:
    __slots__ = ("eng", "fn", "dma", "deps", "ticket", "has_dep", "idx", "dma_total")


class Sched:
    ENGS = ("pe", "act", "dve", "pool", "sp")

    def __init__(self, nc):
        self.nc = nc
        self.ops = []
        self.last_w = {}
        self.readers = {}
        self.dma_count = {}
        self.region_last = {}
        self.alias = {}

    def add(self, eng, fn, reads=(), writes=(), dma=None):
        op = _Op()
        op.eng, op.fn, op.dma = eng, fn, dma
        op.idx = len(self.ops)
        op.has_dep = False
        op.ticket = None
        op.dma_total = None
        cand = {}
        ps_r = [k for k in reads if k[0] == "ps"]
        if ps_r:
            reads = [k for k in reads if k[0] != "ps"]
            writes = list(writes) + [k for k in ps_r if k not in writes]
        for k in reads:
            w = self.last_w.get(k)
            if w is not None:
                cand[w] = "raw"
        for k in writes:
            w = self.last_w.get(k)
            if w is not None:
                cand.setdefault(w, "waw")
            for r in self.readers.get(k, {}).values():
                cand.setdefault(r, "war")
        regions = set(k[0] for k in reads) | set(k[0] for k in writes)
        for r in regions:
            for old in self.alias.get(r, ()):
                for idx in self.region_last.get(old, {}).values():
                    cand.setdefault(idx, "alias")
        best = {}
        for p, kind in cand.items():
            pop = self.ops[p]
            if pop.dma is None and dma is None and pop.eng == eng:
                if eng == "pe":
                    continue
            key = ("d", pop.dma) if pop.dma is not None else ("e", pop.eng)
            if p > best.get(key, -1):
                best[key] = p
        op.deps = sorted(best.values())
        for p in op.deps:
            self.ops[p].has_dep = True
        mykey = ("d", dma) if dma is not None else ("e", eng)
        for k in writes:
            self.last_w[k] = op.idx
            self.readers[k] = {}
        for k in reads:
            if k not in writes:
                self.readers.setdefault(k, {})[mykey] = op.idx
        for r in regions:
            self.region_last.setdefault(r, {})[mykey] = op.idx
        if dma is not None:
            self.dma_count[dma] = self.dma_count.get(dma, 0) + 1
            op.dma_total = 16 * self.dma_count[dma]
        self.ops.append(op)
        return op

    def emit(self):
        nc = self.nc
        cnt = {e: 0 for e in self.ENGS}
        for op in self.ops:
            if op.dma is None and op.has_dep:
                cnt[op.eng] += 1
                op.ticket = cnt[op.eng]
        with ExitStack() as es:
            esem = {e: es.enter_context(nc.semaphore("s_" + e)) for e in self.ENGS}
            dsem = {d: es.enter_context(nc.semaphore("d_" + str(d))) for d in self.dma_count}
            block = es.enter_context(nc.Block())
            ops = self.ops

            def make(engname):
                def body(eng):
                    waited = {}
                    for op in ops:
                        if op.eng != engname:
                            continue
                        need = {}
                        for p in op.deps:
                            pop = ops[p]
                            if pop.dma is not None:
                                key, val = ("d", pop.dma), pop.dma_total
                            else:
                                key, val = ("e", pop.eng), pop.ticket
                            if val > need.get(key, 0):
                                need[key] = val
                        for key, val in need.items():
                            if waited.get(key, 0) >= val:
                                continue
                            waited[key] = val
                            sem = dsem[key[1]] if key[0] == "d" else esem[key[1]]
                            eng.wait_ge(sem, val)
                        ins = op.fn(eng)
                        if op.dma is not None:
                            ins.then_inc(dsem[op.dma], 16)
                        elif op.has_dep:
                            ins.then_inc(esem[engname], 1)
                return body

            block.tensor(make("pe"))
            block.scalar(make("act"))
            block.vector(make("dve"))
            block.gpsimd(make("pool"))
            block.sync(make("sp"))


class Arena:
    def __init__(self, S, tensor, total):
        self.S, self.t, self.total = S, tensor, total
        self.live = {}
        self.dead = []
        self.peak = 0
        self.flat = {}

    def alloc(self, name, nbytes, at=None):
        nbytes = (nbytes + 63) // 64 * 64
        pos = 0
        if at is not None:
            pos = at
            for n_, (o, s) in self.live.items():
                assert not (o < pos + nbytes and pos < o + s), f"{name}@{at} overlaps live {n_} {(o, s)}"
        else:
            for (o, s) in sorted(self.live.values()):
                if o - pos >= nbytes:
                    break
                pos = max(pos, o + s)
        if pos + nbytes > self.total:
            raise RuntimeError(f"arena overflow allocating {name} {nbytes} at {pos}; live={self.live}")
        assert name not in self.live and name not in self.S.alias
        self.live[name] = (pos, nbytes)
        self.peak = max(self.peak, pos + nbytes)
        self.S.alias[name] = [n for (o, s, n) in self.dead if o < pos + nbytes and pos < o + s]
        return pos

    def free(self, *names):
        for name in names:
            o, s = self.live.pop(name)
            self.dead.append((o, s, name))

    def top(self):
        return max(o + s for (o, s) in self.live.values())

    def tile(self, name, free_shape, dt, at=None):
        n = int(np.prod(free_shape))
        sz = 4 if dt == F32 else 2
        off = self.alloc(name, n * sz, at)
        v = self.t[:, off:off + n * sz].bitcast(dt)
        self.flat[name] = (v, dt)
        if len(free_shape) == 2:
            v = v.rearrange("p (a b) -> p a b", a=free_shape[0])
        elif len(free_shape) == 3:
            v = v.rearrange("p (a b c) -> p a b c", a=free_shape[0], b=free_shape[1])
        return v


def build(stop_after=99, dumps=(), p1=(NOWN, 2, NOTH), p1parts=15):
    nc = bass.Bass("TRN2", target_bir_lowering=False)

    def din(name, shape):
        return nc.dram_tensor(name, list(shape), F32, kind="ExternalInput").ap()

    x_own = din("x_own", [NOWN * 128, D])
    x_oth = din("x_oth", [NOTH * 128, D])
    x_halo = din("x_halo", [256, D])
    c_pk = din("c_pk", [128, 8])
    w_ada = din("w_ada", [D, 6 * D])
    bada_pk = din("bada_pk", [128, 48])
    g1_pk = din("g1_pk", [128, 8])
    g2_pk = din("g2_pk", [128, 8])
    gf = din("gf", [D])
    w_in = din("w_in", [D, 3584])
    qn_g = din("qn_g", [64])
    kn_g = din("kn_g", [64])
    sink = din("sink", [1, 8])
    w_br = din("w_br", [2, 512, D])
    w_out = din("w_out", [D, D])
    w1 = din("w1", [D, 4 * D])
    w2 = din("w2", [4 * D, D])
    ropeA_own = din("ropeA_own", [NOWN * 128, 128])
    ropeA_oth = din("ropeA_oth", [NOTH * 128, 128])
    ropeB_own = din("ropeB_own", [NOWN * 128, 128])
    ropeB_halo = din("ropeB_halo", [256, 128])
    masks = din("masks", [4, 128, 512])
    out = nc.dram_tensor("out", [NOWN * 128, D], F32, kind="ExternalOutput").ap()

    ARENA_BYTES = 212736
    with ExitStack() as es:
        arena_t = es.enter_context(nc.sbuf_tensor("arena", [128, ARENA_BYTES], U8))
        psum_t = es.enter_context(nc.psum_tensor("psum", [128, 16384], U8))
        S = Sched(nc)
        A = Arena(S, arena_t, ARENA_BYTES)

        def PS(b0, nb=1, dt=F32):
            return psum_t[:, b0 * 2048:(b0 + nb) * 2048].bitcast(dt)

        def psk(b0, nb=1):
            return [("ps", b) for b in range(b0, b0 + nb)]

        def mm(o, lhsT, rhs, start, stop, reads, writes):
            S.add("pe", lambda e: e.matmul(o, lhsT=lhsT, rhs=rhs, start=start, stop=stop), reads, writes)

        def tr(o, i, ident, reads, writes):
            S.add("pe", lambda e: e.transpose(out=o, in_=i, identity=ident), reads, writes)

        def act(o, i, func, reads, writes, **kw):
            S.add("act", lambda e: e.activation(out=o, in_=i, func=func, **kw), reads, writes)

        def tt(eng, o, i0, i1, op, reads, writes):
            S.add(eng, lambda e: e.tensor_tensor(out=o, in0=i0, in1=i1, op=op), reads, writes)

        def ts(eng, o, i0, s1, s2, op0, op1, reads, writes):
            S.add(eng, lambda e: e.tensor_scalar(out=o, in0=i0, scalar1=s1, scalar2=s2, op0=op0, op1=op1), reads, writes)

        def stt(eng, o, i0, sc, i1, op0, op1, reads, writes):
            S.add(eng, lambda e: e.scalar_tensor_tensor(out=o, in0=i0, scalar=sc, in1=i1, op0=op0, op1=op1), reads, writes)

        def cp(eng, o, i, reads, writes):
            S.add(eng, lambda e: e.tensor_copy(out=o, in_=i), reads, writes)

        def recip(o, i, reads, writes):
            S.add("dve", lambda e: e.reciprocal(out=o, in_=i), reads, writes)

        def memset(eng, o, val, writes):
            S.add(eng, lambda e: e.memset(o, val), (), writes)

        def dma(eng, o, i, sem, reads, writes):
            S.add(eng, lambda e: e.dma_start(out=o, in_=i), reads, writes, dma=sem)

        ident_f = A.tile("ident_f", [128], F32)
        ident_b = A.tile("ident_b", [128], BF16)
        ones_f = A.tile("ones_f", [64], F32)
        modT = A.tile("modT", [48], F32)
        G1c = A.tile("G1c", [8], F32)
        G2c = A.tile("G2c", [8], F32)
        gate1_bc = A.tile("gate1_bc", [D], F32)
        gate2_bc = A.tile("gate2_bc", [D], F32)
        gf_bc = A.tile("gf_bc", [D], F32)
        rstd_own = A.tile("rstd_own", [NOWN], F32)
        stat = A.tile("stat", [128], F32)
        Y_OFF = A.top()
        wa = A.tile("wa", [2, 8, 512], F32)
        X1_OFF = A.top()
        qn_bc = A.tile("qn_bc", [64], F32)
        kn_bc = A.tile("kn_bc", [64], F32)
        sel = A.tile("sel", [192], BF16)
        knsw_bc = A.tile("knsw_bc", [64], F32)
        qnsw_bc = A.tile("qnsw_bc", [64], F32)
        sc = A.tile("sc", [8], F32)
        csb = A.tile("csb", [8], F32)
        bada = A.tile("bada", [48], F32)
        g1c = A.tile("g1c", [8], F32)
        g2c = A.tile("g2c", [8], F32)
        sinkt = A.tile("sinkt", [8], F32)
        esf = A.tile("esf", [8], F32)
        gcolb = A.tile("gcolb", [2, 128], F32)

        memset("pool", ident_f, 0.0, [("ident_f",)])
        S.add("pool", lambda e: e.affine_select(out=ident_f, in_=ident_f, pattern=[[-1, 128]],
                                                compare_op=ALU.not_equal, fill=1.0, base=0,
                                                channel_multiplier=1),
              [("ident_f",)], [("ident_f",)])
        cp("dve", ident_b, ident_f, [("ident_f",)], [("ident_b",)])
        memset("pool", ones_f, 1.0, [("ones_f",)])
        memset("pool", sel, 0.0, [("sel",)])
        memset("pool", sel[:, 64:128], 1.0, [("sel",)])

        dma("sp", csb, c_pk, "c_c", (), [("csb",)])
        dma("sp", bada, bada_pk, "c_bada", (), [("bada",)])
        dma("sp", g1c, g1_pk, "c_g1", (), [("g1c",)])
        act(sc, csb, AF.Silu, [("csb",)], [("sc",)])

        w_ada_v = w_ada.rearrange("(kc p) n -> p kc n", p=128)
        MODBANK = 7
        modps = PS(MODBANK)[:, 320:368]

        def mod_dma(blk):
            sl = blk % 2
            dma("sp", wa[:, sl], w_ada_v[:, :, blk * 512:(blk + 1) * 512], f"wa{sl}", (), [("wa", sl)])

        def mod_block(blk):
            sl = blk % 2
            for jj in range(4):
                j = blk * 4 + jj
                for kc in range(8):
                    mm(modps[:, j:j + 1], wa[:, sl, kc, jj * 128:(jj + 1) * 128], sc[:, kc:kc + 1],
                       kc == 0, kc == 7, [("wa", sl), ("sc",)], psk(MODBANK))

        mod_dma(0)
        mod_dma(1)
        for blk in range(4):
            mod_block(blk)
            mod_dma(blk + 2)
        tt("dve", modT[:, 0:16], modps[:, 0:16], bada[:, 0:16], ALU.add, psk(MODBANK) + [("bada",)], [("modT", "a")])
        S1c = modT[:, 0:8]
        S2c = modT[:, 24:32]
        stt("dve", G1c, modT[:, 8:16], 1.0, g1c, ALU.add, ALU.mult, [("modT", "a"), ("g1c",)], [("G1c",)])

        dma("sp", g2c, g2_pk, "c_g2", (), [("g2c",)])
        dma("sp", sinkt[0:1, :], sink, "c_sink", (), [("sinkt",)])
        dma("sp", gf_bc, gf.partition_broadcast(128), "c_gf", (), [("gf_bc",)])
        dma("sp", qn_bc, qn_g.partition_broadcast(128), "c_qn", (), [("qn_bc",)])
        dma("sp", kn_bc, kn_g.partition_broadcast(128), "c_kn", (), [("kn_bc",)])
        for (src_, dst_, sk, dk) in ((kn_bc, knsw_bc, "kn_bc", "knsw_bc"), (qn_bc, qnsw_bc, "qn_bc", "qnsw_bc")):
            knv = src_.rearrange("p (b t i) -> p b t i", b=2, t=2)
            ksv = dst_.rearrange("p (b t i) -> p b t i", b=2, t=2)
            for t_ in range(2):
                cp("pool", ksv[:, :, t_, :], knv[:, :, 1 - t_, :], [(sk,)], [(dk, t_)])

        KT_B = A.tile("KT_B", [18 * 128], BF16)
        V_B = A.tile("V_B", [18, 192], BF16)
        QT_B = A.tile("QT_B", [4, NOWN * 128], BF16)
        KT_A = A.tile("KT_A", [SEQ], BF16)
        V_A = A.tile("V_A", [64, 192], BF16)
        QT_A = A.tile("QT_A", [4, NOWN * 128], BF16)
        A_END = A.top()
        assert A_END - 16 * 1024 >= X1_OFF + 64 * 1024
        memset("pool", V_A[:, :, 64:128], 1.0, [("V_A",)])
        memset("pool", V_B[:, :, 64:128], 1.0, [("V_B",)])

        Wb = A.tile("Wb", [8, 1536], BF16)
        dma("pool", Wb, w_in.rearrange("(kc p) n -> p kc n", p=128)[:, :, 0:1536], "w_in", (), [("Wb",)])
        xs = A.tile("xs", [2, D], F32)
        rp = A.tile("rp", [3, 256], F32)
        rpg = A.tile("rpg", [3, 4, 64], F32)
        xn = A.tile("xn", [1, D], BF16)
        hT = A.tile("hT", [2, 8, 128], BF16)
        wkq = A.tile("wkq", [2, 3, 512], F32)
        wkk = A.tile("wkk", [3, 3, 128], F32)
        qrq = A.tile("qrq", [6, 512], BF16)
        qrk = A.tile("qrk", [6, 128], BF16)

        tiles = ([("own", t) for t in range(p1[0])] + [("halo", h) for h in range(p1[1])]
                 + [("oth", u) for u in range(p1[2])])
        NT = len(tiles)

        def stageA(ti):
            kind, idx = tiles[ti]
            s3, s2 = ti % 3, ti % 2
            sx = ti % 2
            if kind == "own":
                xa = x_own[idx * 128:(idx + 1) * 128, :]
                ropes = [ropeA_own[idx * 128:(idx + 1) * 128, :], ropeB_own[idx * 128:(idx + 1) * 128, :]]
            elif kind == "oth":
                xa = x_oth[idx * 128:(idx + 1) * 128, :]
                ropes = [ropeA_oth[idx * 128:(idx + 1) * 128, :]]
            else:
                xa = x_halo[idx * 128:(idx + 1) * 128, :]
                ropes = [ropeB_halo[idx * 128:(idx + 1) * 128, :]]
            dma("sp", xs[:, sx], xa, f"xs{sx}", (), [("xs", sx)])
            off = 0
            for rap in ropes:
                dma("sp", rp[:, s3, off:off + 128], rap, f"rp{s3}_{off}", (), [("rp", s3, off)])
                off += 128
            ssc = stat[:, s2:s2 + 1]
            S.add("dve", lambda e, o_=xn[:, 0], i_=xs[:, sx], a_=ssc: e.scalar_tensor_tensor(
                out=o_, in0=i_, scalar=1.0, in1=i_, op0=ALU.mult, op1=ALU.mult, accum_out=a_),
                [("xs", sx)], [("xn",), ("stat", "ss", s2)])
            rsc = stat[:, 2 + s2:3 + s2]
            act(rsc, ssc, AF.Ln, [("stat", "ss", s2)], [("stat", "rs", s2)], scale=1.0 / D, bias=EPS)
            if kind == "own":
                rstd, rkey = rstd_own[:, idx:idx + 1], ("rstd_own", idx)
            else:
                rstd, rkey = stat[:, 4 + s2:5 + s2], ("stat", "rstd", s2)
            act(rstd, rsc, AF.Exp, [("stat", "rs", s2)], [rkey], scale=-0.5)
            act(xn[:, 0], xs[:, sx], AF.Copy, [("xs", sx), rkey], [("xn",)], scale=rstd)
            if kind != "halo":
                tt("pool", rpg[:, s3, 0, :], rp[:, s3, 0:64], kn_bc, ALU.mult, [("rp", s3, 0), ("kn_bc",)],
                   [("rpg", s3, 0)])
                tt("pool", rpg[:, s3, 1, :], rp[:, s3, 64:128], knsw_bc, ALU.mult,
                   [("rp", s3, 0), ("knsw_bc", 0), ("knsw_bc", 1)], [("rpg", s3, 1)])
            if kind == "own":
                tt("pool", rpg[:, s3, 2, :], rp[:, s3, 0:64], qn_bc, ALU.mult, [("rp", s3, 0), ("qn_bc",)],
                   [("rpg", s3, 2)])
                tt("pool", rpg[:, s3, 3, :], rp[:, s3, 64:128], qnsw_bc, ALU.mult,
                   [("rp", s3, 0), ("qnsw_bc", 0), ("qnsw_bc", 1)], [("rpg", s3, 3)])

        def hbank(kc):
            return (0, kc * 128) if kc < 2 else (1, (kc - 2) * 128)

        def stageB(ti):
            s2 = ti % 2
            for kc in range(8):
                bk, c0 = hbank(kc)
                tr(PS(bk, 1, BF16)[:, c0:c0 + 128], xn[:, 0, kc * 128:(kc + 1) * 128], ident_b,
                   [("xn",), ("ident_b",)], psk(bk))
            for kc in range(8):
                bk, c0 = hbank(kc)
                o = hT[:, s2, kc, :]
                i = PS(bk, 1, BF16)[:, c0:c0 + 128]
                if bk == 0:
                    act(o, i, AF.Identity, psk(bk) + [("G1c",), ("modT", "a")], [("hT", s2, kc)],
                        scale=G1c[:, kc:kc + 1], bias=S1c[:, kc:kc + 1])
                else:
                    ts("dve", o, i, G1c[:, kc:kc + 1], S1c[:, kc:kc + 1], ALU.mult, ALU.add,
                       psk(bk) + [("G1c",), ("modT", "a")], [("hT", s2, kc)])

        def proj(s2, c0, c1, bank):
            for kc in range(8):
                mm(PS(bank)[:, 0:c1 - c0], hT[:, s2, kc, :], Wb[:, kc, c0:c1], kc == 0, kc == 7,
                   [("hT", s2, kc), ("Wb",)], psk(bank))

        def chain(ps_view, bank, H, norm, gains, tabC, tabS, tab_keys, nb, wf, wkey, scol, qr_v, qr_key,
                  trbank, trcol, final):
            n = H * 64
            hw = 32 // nb
            v3 = lambda a: a.rearrange("p (h d) -> p h d", h=H)
            f0, f1, f2 = wf[0][:, 0:n], wf[1][:, 0:n], wf[2][:, 0:n]
            k0, k1, k2 = wkey + (0,), wkey + (1,), wkey + (2,)
            act(f0, ps_view, AF.Copy, psk(bank), [k0])
            yield
            tt("pool", v3(f2), v3(f0), tabC.unsqueeze(1).to_broadcast([128, H, 64]), ALU.mult,
               [k0] + tab_keys, [k2])
            if norm:
                ssh = stat[:, scol:scol + H]
                rh = stat[:, scol + 8:scol + 8 + H]
                for h_ in range(H):
                    act(f1[:, h_ * 64:(h_ + 1) * 64], ps_view[:, h_ * 64:(h_ + 1) * 64], AF.Square, psk(bank),
                        [k1, ("stat", "ssh", scol)], accum_out=ssh[:, h_:h_ + 1])
            yield
            if norm:
                act(rh, ssh, AF.Ln, [("stat", "ssh", scol)], [("stat", "rh", scol)], scale=1.0 / 64, bias=EPS)
                act(rh, rh, AF.Exp, [("stat", "rh", scol)], [("stat", "rh", scol)], scale=-0.5)
            sv = f0.rearrange("p (h b t i) -> p h b t i", h=H, b=nb, t=2)
            bv = f1.rearrange("p (h b t i) -> p h b t i", h=H, b=nb, t=2)
            Sv = tabS.rearrange("p (b t i) -> p b t i", b=nb, t=2)
            for t_ in range(2):
                tt("pool", bv[:, :, :, t_, :], sv[:, :, :, 1 - t_, :],
                   Sv[:, :, t_, :].unsqueeze(1).to_broadcast([128, H, nb, hw]), ALU.mult,
                   [k0] + tab_keys, [k1])
            yield
            if H == 8:
                p4 = lambda a: a.rearrange("p (g j d) -> p g j d", g=2, j=4)
                va, vb = p4(f2), p4(f1)
                ov = qr_v.rearrange("p (j g d) -> p g j d", j=4, g=2)
            else:
                va, vb, ov = v3(f2), v3(f1), v3(qr_v)
            if norm:
                tt("dve", va, va, vb, ALU.add, [k1, k2], [k2])
                yield
                if H == 8:
                    rb = rh.rearrange("p (g j) -> p g j", g=2).unsqueeze(3).to_broadcast([128, 2, 4, 64])
                else:
                    rb = rh.unsqueeze(2).to_broadcast([128, H, 64])
                tt("dve", ov, va, rb, ALU.mult, [k2, ("stat", "rh", scol)], [qr_key])
            else:
                tt("dve", ov, va, vb, ALU.add, [k1, k2], [qr_key])
            yield

        def chain_tail(H, qr_v, qr_key, trbank, trcol, final):
            n = H * 64
            pTv = PS(trbank, 1, BF16)
            for j in range(n // 128):
                tr(pTv[:, trcol + j * 128:trcol + (j + 1) * 128], qr_v[:, j * 128:(j + 1) * 128], ident_b,
                   [qr_key, ("ident_b",)], psk(trbank))
            dst, dkey = final
            if n == 512:
                cp("dve", dst, pTv[:, trcol:trcol + 512].rearrange("p (j q) -> p j q", j=4), psk(trbank), [dkey])
            else:
                cp("dve", dst, pTv[:, trcol:trcol + 128], psk(trbank), [dkey])

        def run_chains(gens):
            gens = list(gens)
            while gens:
                for g_ in list(gens):
                    try:
                        next(g_)
                    except StopIteration:
                        gens.remove(g_)

        def vcopy(dst3, bank, vkey):
            cp("dve", dst3.rearrange("p (a d) -> p a d", a=3)[:, 0:3:2, :],
               PS(bank)[:, 128:256].rearrange("p (a d) -> p a d", a=2), psk(bank), [vkey])

        def stageC(ti):
            kind, idx = tiles[ti]
            s3, s2 = ti % 3, ti % 2
            rA = rp[:, s3, 0:128]
            rB = rp[:, s3, 128:256] if kind == "own" else rp[:, s3, 0:128]
            kA = [("rp", s3, 0)]
            kB = [("rp", s3, 128)] if kind == "own" else [("rp", s3, 0)]
            kG = [("rpg", s3, 0), ("rpg", s3, 1)]
            gens = []
            tails = []
            par = ti % 3

            def add(ps_view, bank, H, norm, tabC, tabS, tkeys, nb, wslot, wk_t, wname, scol, qr_t, qname, qslot,
                    trbank, trcol, final):
                wf = [wk_t[:, wslot, f, :] for f in range(3)]
                qv = qr_t[:, qslot, :]
                gens.append(chain(ps_view, bank, H, norm, None, tabC, tabS, tkeys, nb, wf, (wname, wslot), scol,
                                  qv, (qname, qslot), trbank, trcol, final))
                tails.append(lambda: chain_tail(H, qv, (qname, qslot), trbank, trcol, final))

            if kind == "own":
                t = idx
                proj(s2, 0, 512, 2)
                proj(s2, 512, 768, 3)
                proj(s2, 768, 1280, 4)
                proj(s2, 1280, 1536, 5)
                vcopy(V_A[:, t, :], 3, ("V_A",))
                vcopy(V_B[:, t + 1, :], 5, ("V_B",))
                kQ = [("rpg", s3, 2), ("rpg", s3, 3)]
                add(PS(2), 2, 8, True, rpg[:, s3, 2, :], rpg[:, s3, 3, :], kQ, 2, 0, wkq, "wkq", 8,
                    qrq, "qrq", 0 + par, 6, 0, (QT_A[:, :, t * 128:(t + 1) * 128], ("QT_A",)))
                add(PS(3)[:, 0:128], 3, 2, True, rpg[:, s3, 0, :], rpg[:, s3, 1, :], kG, 2, 0, wkk, "wkk", 24,
                    qrk, "qrk", 0 + par, 6, 512, (KT_A[:, t * 128:(t + 1) * 128], ("KT_A",)))
                add(PS(4), 4, 8, False, rB[:, 0:64], rB[:, 64:128], kB, 1, 1, wkq, "wkq", 0,
                    qrq, "qrq", 3 + par, 7, 0, (QT_B[:, :, t * 128:(t + 1) * 128], ("QT_B",)))
                add(PS(5)[:, 0:128], 5, 2, False, rB[:, 0:64], rB[:, 64:128], kB, 1, 1, wkk, "wkk", 0,
                    qrk, "qrk", 3 + par, 7, 512, (KT_B[:, (t + 1) * 128:(t + 2) * 128], ("KT_B",)))
            elif kind == "oth":
                kt = NOWN + idx
                bank = 2 + ti % 4
                w_ = (0, 2)[ti % 2]
                proj(s2, 512, 768, bank)
                vcopy(V_A[:, kt, :], bank, ("V_A",))
                add(PS(bank)[:, 0:128], bank, 2, True, rpg[:, s3, 0, :], rpg[:, s3, 1, :], kG, 2, w_, wkk, "wkk",
                    24 + 16 * (ti % 2), qrk, "qrk", 0 + par, 6 + ti % 2, 0,
                    (KT_A[:, kt * 128:(kt + 1) * 128], ("KT_A",)))
            else:
                kb = 0 if idx == 0 else 17
                proj(s2, 1280, 1536, 5)
                vcopy(V_B[:, kb, :], 5, ("V_B",))
                add(PS(5)[:, 0:128], 5, 2, False, rB[:, 0:64], rB[:, 64:128], kB, 1, 1, wkk, "wkk", 0,
                    qrk, "qrk", 3 + par, 7, 512, (KT_B[:, kb * 128:(kb + 1) * 128], ("KT_B",)))
            run_chains(gens)
            return tails

        MOD_EVERY = 6
        if stop_after >= 1:
            next_blk = 4
            tailq = [[], []]
            for step in range(NT + 4):
                for tl in tailq.pop(0):
                    tl()
                if 1 <= step <= NT:
                    stageB(step - 1)
                if step < NT:
                    stageA(step)
                new_tails = []
                if 2 <= step <= NT + 1:
                    new_tails = stageC(step - 2)
                tailq.append(new_tails)
                if step >= 2 and step % MOD_EVERY == 0 and next_blk < 12:
                    mod_block(next_blk)
                    if next_blk + 2 < 12:
                        mod_dma(next_blk + 2)
                    next_blk += 1
            while next_blk < 12:
                mod_block(next_blk)
                if next_blk + 2 < 12:
                    mod_dma(next_blk + 2)
                next_blk += 1
        else:
            for blk in range(4, 12):
                mod_block(blk)
                if blk + 2 < 12:
                    mod_dma(blk + 2)
        tt("dve", modT[:, 16:48], modps[:, 16:48], bada[:, 16:48], ALU.add, psk(MODBANK) + [("bada",)], [("modT", "b")])
        stt("dve", G2c, modT[:, 32:40], 1.0, g2c, ALU.add, ALU.mult, [("modT", "b"), ("g2c",)], [("G2c",)])
        act(esf[0:1, :], sinkt[0:1, :], AF.Exp, [("sinkt",)], [("esf",)])
        for gi, (gbc, base, bank) in enumerate(((gate1_bc, 16, 1), (gate2_bc, 40, 3))):
            gps = PS(bank, 2)
            for cc in range(8):
                sl = cc % 2
                cp("dve", gcolb[:, sl], modT[:, base + cc:base + cc + 1].to_broadcast([128, 128]),
                   [("modT", "b")], [("gcolb", sl)])
                mm(gps[:, cc * 128:(cc + 1) * 128], gcolb[:, sl], ident_f, True, True,
                   [("gcolb", sl), ("ident_f",)], psk(bank + cc // 4))
            cp("dve", gbc, gps, psk(bank, 2), [("gate_bc", gi)])
        A.free("Wb", "xs", "rp", "rpg", "xn", "hT", "wkq", "wkk", "qrq", "qrk", "wa",
               "sc", "csb", "bada", "g1c", "g2c", "sinkt", "gcolb", "knsw_bc", "qnsw_bc")
        yaT = A.tile("yaT", [4, NOWN * 128], BF16, at=Y_OFF)
        ybT = A.tile("ybT", [4, NOWN * 128], BF16, at=Y_OFF + 16 * 1024)
        maskb = A.tile("maskb", [4, 512], BF16, at=ARENA_BYTES - 4096)
        esink = A.tile("esink", [8, 128], BF16, at=ARENA_BYTES - 4096 - 2048)
        dma("pool", maskb, masks.rearrange("m p n -> p m n"), "c_mask", (), [("maskb",)])
        cp("dve", esink[0:1, :, :], esf[0:1, :].unsqueeze(2).to_broadcast([1, 8, 128]), [("esf",)], [("esink",)])
        A.free("esf")
        Wg = A.tile("Wg", [8, 2048], BF16)
        if stop_after >= 4:
            dma("pool", Wg, w_in.rearrange("(kc p) n -> p kc n", p=128)[:, :, 1536:3584], "w_g", (), [("Wg",)])

        SCALE = 0.125
        PT = A.tile("PT", [3, 1024], BF16)
        osb = A.tile("osb", [2, 2, 512], F32)
        rrow = A.tile("rrow", [2, 512], F32)
        P2_END = max(A.live[n_][0] + A.live[n_][1] for n_ in ("Wg", "PT", "osb", "rrow"))

        def norm_part1(qt, par, obanks, gsel):
            for g in gsel:
                r = 64 if g == 0 else 0
                cp("dve", osb[:, par, g, :], PS(obanks[g]), psk(obanks[g]), [("osb", par, g)])
                recip(rrow[r:r + 1, par, :], osb[r:r + 1, par, g, :],
                      [("osb", par, g)], [("rrow", par, g)])

        def norm_part2(qt, par, bcbanks, gsel, dstT, dkey):
            for g in gsel:
                r = 64 if g == 0 else 0
                bcbank = bcbanks[g]
                mm(PS(bcbank)[g * 64:(g + 1) * 64, :], ones_f[r:r + 1, 0:64],
                   rrow[r:r + 1, par, :], True, True,
                   [("rrow", par, g), ("ones_f",)], psk(bcbank))
                tt("dve", dstT[g * 64:(g + 1) * 64, :, qt * 128:(qt + 1) * 128],
                   osb[g * 64:(g + 1) * 64, par, g, :].rearrange("p (j q) -> p j q", j=4),
                   PS(bcbank)[g * 64:(g + 1) * 64, :].rearrange("p (j q) -> p j q", j=4), ALU.mult,
                   [("osb", par, g)] + psk(bcbank), [dkey])

        if stop_after >= 2:
            NKT = 64
            AHEAD = 2

            def qk2(it):
                qt, kt = divmod(it, NKT)
                sb_ = it % 3
                for g in range(2):
                    mm(PS(2 * sb_ + g), KT_A[g * 64:(g + 1) * 64, kt * 128:(kt + 1) * 128],
                       QT_A[g * 64:(g + 1) * 64, :, qt * 128:(qt + 1) * 128], True, True,
                       [("KT_A",), ("QT_A",)], psk(2 * sb_ + g))

            deferred = {}
            total = NOWN * NKT
            for it in range(min(AHEAD, total)):
                qk2(it)
            for it in range(total):
                qt, kt = divmod(it, NKT)
                sb_, pb = it % 3, it % 3
                act(PT[:, pb, :], PS(2 * sb_, 2), AF.Exp, psk(2 * sb_, 2), [("PT", pb)], scale=SCALE)
                if it + AHEAD < total:
                    qk2(it + AHEAD)
                for g in range(2):
                    mm(PS(6 + g), V_A[:, kt, g * 64:g * 64 + 128], PT[:, pb, g * 512:(g + 1) * 512],
                       kt == 0, kt == NKT - 1, [("PT", pb), ("V_A",)], psk(6 + g))
                if kt == NKT - 1:
                    par = qt % 2
                    norm_part1(qt, par, (6, 7), (0, 1))
                    deferred[it + 2] = (lambda qt=qt, par=par: norm_part2(qt, par, (0, 2), (0, 1), yaT, ("yaT",)))
                if it in deferred:
                    deferred.pop(it)()
            for k in sorted(deferred):
                deferred[k]()
        A.free("PT", "KT_A", "V_A", "QT_A")
        Wo = A.tile("Wo", [8, D], BF16, at=A_END - 16 * 1024)
        WbrA = A.tile("WbrA", [4, D], BF16, at=P2_END + 8 * 1024)
        WbrB = A.tile("WbrB", [4, D], BF16, at=P2_END)
        if stop_after >= 4:
            for bi, (wt, nm) in enumerate(((WbrA, "WbrA"), (WbrB, "WbrB"))):
                src = w_br[bi].rearrange("(g j d) n -> g d j n", g=2, j=4)
                for g in range(2):
                    dma("pool", wt[g * 64:(g + 1) * 64, :, :], src[g], "w_" + nm, (), [(nm, g)])
            dma("pool", Wo, w_out.rearrange("(kc p) n -> p kc n", p=128), "w_o", (), [("Wo",)])

        PTB = A.tile("PTB", [2, 1536], BF16)
        if stop_after >= 3:
            def qk3(i):
                qt, g = divmod(i, 2)
                sb_ = i % 2
                for blk in range(3):
                    kb = qt + blk
                    bank = 3 * sb_ + blk
                    mm(PS(bank), KT_B[g * 64:(g + 1) * 64, kb * 128:(kb + 1) * 128],
                       QT_B[g * 64:(g + 1) * 64, :, qt * 128:(qt + 1) * 128], True, blk == 1,
                       [("KT_B",), ("QT_B",)], psk(bank))
                for blk in (0, 2):
                    bank = 3 * sb_ + blk
                    mi = (0 if qt == 0 else 1) if blk == 0 else (3 if qt == NOWN - 1 else 2)
                    mm(PS(bank), ident_b, maskb[:, mi, :], False, True,
                       [("ident_b",), ("maskb",)], psk(bank))

            total = NOWN * 2
            deferred = {}
            qk3(0)
            for i in range(total):
                qt, g = divmod(i, 2)
                sb_ = i % 2
                act(PTB[:, sb_, :], PS(3 * sb_, 3), AF.Exp, psk(3 * sb_, 3), [("PTB", sb_)], scale=SCALE)
                if i + 1 < total:
                    qk3(i + 1)
                for blk in range(3):
                    kb = qt + blk
                    mm(PS(6), V_B[:, kb, g * 64:g * 64 + 128], PTB[:, sb_, blk * 512:(blk + 1) * 512],
                       blk == 0, False, [("PTB", sb_), ("V_B",)], psk(6))
                mm(PS(6), sel[0:1, (0 if g == 0 else 64):(128 if g == 0 else 192)],
                   esink[0:1, g * 4:(g + 1) * 4, :].rearrange("p j q -> p (j q)"), False, True,
                   [("sel",), ("esink",)], psk(6))
                par = i % 2
                norm_part1(qt, par, (6, 6), (g,))
                deferred[i + 1] = (lambda qt=qt, par=par, g=g: norm_part2(qt, par, (7, 7), (g,), ybT, ("ybT",)))
                if i in deferred:
                    deferred.pop(i)()
            for k in sorted(deferred):
                deferred[k]()
        A.free("PTB", "osb", "rrow", "KT_B", "V_B", "QT_B",
               "qn_bc", "kn_bc", "maskb", "sel", "esink")

        x1 = A.tile("x1", [NOWN, D], F32, at=X1_OFF)
        xs = A.tile("xs4", [2, D], F32)
        xn = A.tile("xn4", [1, D], BF16)
        hT = A.tile("hT4", [1, 8, 128], BF16)
        sg = A.tile("sg", [D], F32)
        m12 = A.tile("m12", [D], F32)
        mg = A.tile("mg", [D], BF16)
        mT = A.tile("mT", [8, 128], BF16)
        tmp = A.tile("tmp4", [D], F32)
        if stop_after >= 4:
            def front4(t):
                s2 = t % 2
                dma("sp", xs[:, s2], x_own[t * 128:(t + 1) * 128, :], f"x4_{s2}", (), [("xs4", s2)])
                act(xn[:, 0], xs[:, s2], AF.Copy, [("xs4", s2), ("rstd_own", t)], [("xn4",)],
                    scale=rstd_own[:, t:t + 1])
                pT = PS(6, 1, BF16)
                for kc in range(8):
                    tr(pT[:, kc * 128:(kc + 1) * 128], xn[:, 0, kc * 128:(kc + 1) * 128], ident_b,
                       [("xn4",), ("ident_b",)], psk(6))
                for kc in range(8):
                    o = hT[:, 0, kc, :]
                    i = pT[:, kc * 128:(kc + 1) * 128]
                    act(o, i, AF.Identity, psk(6) + [("G1c",), ("modT", "a")], [("hT4", kc)],
                        scale=G1c[:, kc:kc + 1], bias=S1c[:, kc:kc + 1])

            def mid4(t):
                s2 = t % 2
                for bi, (wt, nm, srcT, skey, dst, dkey) in enumerate((
                        (WbrA, "WbrA", yaT, ("yaT",), m12, ("m12",)),
                        (WbrB, "WbrB", ybT, ("ybT",), tmp, ("tmp4",)))):
                    for c in range(2):
                        c0 = bi * 1024 + c * 512
                        for kc in range(8):
                            mm(PS(c), hT[:, 0, kc, :], Wg[:, kc, c0:c0 + 512], kc == 0, kc == 7,
                               [("hT4", kc), ("Wg",)], psk(c))
                    act(sg, PS(0, 2), AF.Sigmoid, psk(0, 2), [("sg",)])
                    for c in range(2):
                        for j in range(4):
                            mm(PS(2 + 2 * bi + c), srcT[:, j, t * 128:(t + 1) * 128], wt[:, j, c * 512:(c + 1) * 512],
                               j == 0, j == 3, [skey, (nm, 0), (nm, 1)], psk(2 + 2 * bi + c))
                    tt("dve", dst, sg, PS(2 + 2 * bi, 2), ALU.mult, [("sg",)] + psk(2 + 2 * bi, 2), [dkey])

            def tail4(t):
                s2 = t % 2
                tt("pool", mg, m12, tmp, ALU.add, [("m12",), ("tmp4",)], [("mg",)])
                pT2 = PS(7, 1, BF16)
                for kc in range(8):
                    tr(pT2[:, kc * 128:(kc + 1) * 128], mg[:, kc * 128:(kc + 1) * 128], ident_b,
                       [("mg",), ("ident_b",)], psk(7))
                cp("dve", mT, pT2.rearrange("p (k q) -> p k q", k=8), psk(7), [("mT",)])
                for c in range(2):
                    for kc in range(8):
                        mm(PS(c), mT[:, kc, :], Wo[:, kc, c * 512:(c + 1) * 512], kc == 0, kc == 7,
                           [("mT",), ("Wo",)], psk(c))
                tt("dve", tmp, PS(0, 2), gate1_bc, ALU.mult, psk(0, 2) + [("gate_bc", 0)], [("tmp4",)])
                tt("pool", x1[:, t, :], tmp, xs[:, s2], ALU.add, [("tmp4",), ("xs4", s2)], [("x1", t)])

            front4(0)
            for t in range(NOWN):
                mid4(t)
                if t + 1 < NOWN:
                    front4(t + 1)
                tail4(t)
        A.free("Wg", "WbrA", "WbrB", "Wo", "xs4", "xn4", "hT4", "sg", "m12", "mg", "mT", "tmp4", "yaT", "ybT")

        h2T = A.tile("h2T", [8, NOWN * 128], BF16)
        W1q = A.tile("W1q", [2, 8, 1024], BF16)
        W2q = A.tile("W2q", [2, 8, 1024], BF16)
        hidT = A.tile("hidT", [2, 8, 512], BF16)
        rbuf = A.tile("rbuf", [2, 512], F32)
        xn = A.tile("xn5", [1, D], BF16)
        sqj = A.tile("sqj5", [D], BF16)
        tmp = A.tile("tmp5", [1, D], F32)
        obuf = A.tile("obuf", [1, D], F32)
        if stop_after >= 5:
            w1v = w1.rearrange("(kc p) n -> p kc n", p=128)
            w2v = w2.rearrange("(hc p) n -> p hc n", p=128)

            def load_q(qh):
                qs_ = qh % 2
                dma("pool", W1q[:, qs_], w1v[:, :, qh * 1024:(qh + 1) * 1024], f"w1_{qs_}", (), [("W1q", qs_)])
                dma("pool", W2q[:, qs_], w2v[:, qh * 8:(qh + 1) * 8, :], f"w2_{qs_}", (), [("W2q", qs_)])

            load_q(0)
            load_q(1)
            for t in range(NOWN):
                s2 = t % 2
                ssc = stat[:, 64 + s2:65 + s2]
                act(sqj, x1[:, t, :], AF.Square, [("x1", t)], [("sqj5",), ("stat", "ss5", s2)], accum_out=ssc)
                rsc = stat[:, 66 + s2:67 + s2]
                act(rsc, ssc, AF.Sqrt, [("stat", "ss5", s2)], [("stat", "rs5", s2)], scale=1.0 / D, bias=EPS)
                recip(rsc, rsc, [("stat", "rs5", s2)], [("stat", "rs5", s2)])
                act(xn[:, 0], x1[:, t, :], AF.Copy, [("x1", t), ("stat", "rs5", s2)], [("xn5",)], scale=rsc)
                for kc in range(8):
                    bk = 6 + kc // 4
                    tr(PS(bk, 1, BF16)[:, (kc % 4) * 128:(kc % 4 + 1) * 128], xn[:, 0, kc * 128:(kc + 1) * 128],
                       ident_b, [("xn5",), ("ident_b",)], psk(bk))
                for kc in range(8):
                    bk = 6 + kc // 4
                    o = h2T[:, kc, t * 128:(t + 1) * 128]
                    i = PS(bk, 1, BF16)[:, (kc % 4) * 128:(kc % 4 + 1) * 128]
                    if bk == 6:
                        act(o, i, AF.Identity, psk(bk) + [("G2c",), ("modT", "b")], [("h2T", t // 4, kc)],
                            scale=G2c[:, kc:kc + 1], bias=S2c[:, kc:kc + 1])
                    else:
                        ts("dve", o, i, G2c[:, kc:kc + 1], S2c[:, kc:kc + 1], ALU.mult, ALU.add,
                           psk(bk) + [("G2c",), ("modT", "b")], [("h2T", t // 4, kc)])

            hcount = [0]

            def hid(qh, grp):
                qs_ = qh % 2
                hs = (qh * 4 + grp) % 2
                for hc in range(8):
                    hb = hcount[0] % 4
                    rb = hcount[0] % 2
                    hcount[0] += 1
                    for kc in range(8):
                        mm(PS(hb), W1q[:, qs_, kc, hc * 128:(hc + 1) * 128], h2T[:, kc, grp * 512:(grp + 1) * 512],
                           kc == 0, kc == 7, [("W1q", qs_), ("h2T", grp, kc)], psk(hb))
                    act(rbuf[:, rb, :], PS(hb), AF.Relu, psk(hb), [("rbuf", rb)])
                    tt("pool", hidT[:, hs, hc, :], rbuf[:, rb, :], rbuf[:, rb, :], ALU.mult,
                       [("rbuf", rb)], [("hidT", hs, hc)])

            ycount = [0]

            def ymm(qh, grp):
                qs_ = qh % 2
                hs = (qh * 4 + grp) % 2
                for tt_ in range(4):
                    t = grp * 4 + tt_
                    yb = 4 + 2 * (ycount[0] % 2)
                    ys = ycount[0] % 2
                    ycount[0] += 1
                    for c in range(2):
                        for hc in range(8):
                            mm(PS(yb + c), hidT[:, hs, hc, tt_ * 128:(tt_ + 1) * 128],
                               W2q[:, qs_, hc, c * 512:(c + 1) * 512], hc == 0, hc == 7,
                               [("hidT", hs, hc), ("W2q", qs_)], psk(yb + c))
                    tt("dve", tmp[:, 0, :], PS(yb, 2), gate2_bc, ALU.mult, psk(yb, 2) + [("gate_bc", 1)],
                       [("tmp5",)])
                    tt("pool", x1[:, t, :], x1[:, t, :], tmp[:, 0, :], ALU.add, [("x1", t), ("tmp5",)],
                       [("x1", t)])
                    if qh == 3:
                        ssc = stat[:, 68 + ys:69 + ys]
                        act(sqj, x1[:, t, :], AF.Square, [("x1", t)], [("sqj5",), ("stat", "ss6", ys)],
                            accum_out=ssc)
                        rsc = stat[:, 70 + ys:71 + ys]
                        act(rsc, ssc, AF.Sqrt, [("stat", "ss6", ys)], [("stat", "rs6", ys)], scale=1.0 / D, bias=EPS)
                        recip(rsc, rsc, [("stat", "rs6", ys)], [("stat", "rs6", ys)])
                        stt("dve", obuf[:, 0, :], x1[:, t, :], rsc, gf_bc, ALU.mult, ALU.mult,
                            [("x1", t), ("stat", "rs6", ys), ("gf_bc",)], [("obuf",)])
                        dma("sp", out[t * 128:(t + 1) * 128, :], obuf[:, 0, :], "o_0", [("obuf",)],
                            [("out", t)])

            seq = [(qh, grp) for qh in range(4) for grp in range(4)]
            hid(*seq[0])
            for n_, (qh, grp) in enumerate(seq):
                if n_ + 1 < len(seq):
                    hid(*seq[n_ + 1])
                ymm(qh, grp)
                if grp == 3 and qh + 2 < 4:
                    load_q(qh + 2)
            S.add("sp", lambda e: e.nop(), [("out", t) for t in range(NOWN)], ())

        for (nm, keys) in dumps:
            fv, dt_ = A.flat[nm]
            d_ap = nc.dram_tensor("dbg_" + nm, [128, fv.shape[1]], F32, kind="ExternalOutput").ap()
            dma("pool", d_ap, fv, "dbg_" + nm, list(keys), [("dbgout", nm)])
            S.add("sp", lambda e: e.nop(), [("dbgout", nm)], ())
        S.emit()
        nc._arena_peak = A.peak
    return nc


def _rope_tables():
    t = np.arange(SEQ)
    row = (t // 64).astype(np.float32)
    col = (t % 64).astype(np.float32)
    inv16 = (10000.0 ** (-np.arange(0, 32, 2, dtype=np.float32) / 32)).astype(np.float32)
    inv32 = (10000.0 ** (-np.arange(0, 64, 2, dtype=np.float32) / 64)).astype(np.float32)
    A_ = np.zeros((SEQ, 128), np.float32)
    for b, pos in enumerate((row, col)):
        ang = pos[:, None] * inv16[None, :]
        cs, sn = np.cos(ang).astype(np.float32), np.sin(ang).astype(np.float32)
        A_[:, b * 32:b * 32 + 16] = cs
        A_[:, b * 32 + 16:b * 32 + 32] = cs
        A_[:, 64 + b * 32:64 + b * 32 + 16] = -sn
        A_[:, 64 + b * 32 + 16:64 + b * 32 + 32] = sn
    B_ = np.zeros((SEQ, 128), np.float32)
    ang = t.astype(np.float32)[:, None] * inv32[None, :]
    cs, sn = np.cos(ang).astype(np.float32), np.sin(ang).astype(np.float32)
    B_[:, 0:32] = cs
    B_[:, 32:64] = cs
    B_[:, 64:96] = -sn
    B_[:, 96:128] = sn
    return A_, B_


def _masks(core):
    kl = np.arange(128)[:, None]
    ql = np.arange(128)[None, :]
    prev = np.where(kl >= ql, 0.0, NEG).astype(np.float32)
    nxt = np.where(kl <= ql, 0.0, NEG).astype(np.float32)
    allneg = np.full((128, 128), NEG, np.float32)
    first = allneg if core % 4 == 0 else prev
    last = allneg if core % 4 == 3 else nxt
    return np.stack([np.tile(m, (1, 4)) for m in (first, prev, nxt, last)]).astype(np.float32)


def make_in_maps(x, c, w_ada, b_ada, norm1_g, w_in, q_norm_a, k_norm_a, sink_b,
                 w_branch, w_out, norm2_g, w_mlp_in, w_mlp_out, final_g):
    f = lambda a: np.ascontiguousarray(np.asarray(a, dtype=np.float32))
    x = f(x); c = f(c)
    ropeA, ropeB = _rope_tables()
    pk = lambda v: f(np.asarray(v, np.float32).reshape(-1, 128).T)
    shared = {
        "w_ada": f(w_ada[0]), "bada_pk": pk(b_ada[0]), "g1_pk": pk(norm1_g[0]), "g2_pk": pk(norm2_g[0]),
        "gf": f(final_g), "w_in": f(w_in[0]), "qn_g": f(q_norm_a[0]), "kn_g": f(k_norm_a[0]),
        "sink": f(np.asarray(sink_b[0]).reshape(1, 8)), "w_br": f(w_branch[0]), "w_out": f(w_out[0]),
        "w1": f(w_mlp_in[0]), "w2": f(w_mlp_out[0]),
    }
    in_maps = []
    for core in range(N_CORES):
        b, qi = divmod(core, 4)
        q0 = qi * 2048
        own = slice(q0, q0 + 2048)
        oth_idx = np.concatenate([np.arange(0, q0), np.arange(q0 + 2048, SEQ)])
        halo = np.zeros((256, D), np.float32)
        ropeB_h = np.zeros((256, 128), np.float32)
        if q0 > 0:
            halo[0:128] = x[b, q0 - 128:q0]
            ropeB_h[0:128] = ropeB[q0 - 128:q0]
        if q0 + 2048 < SEQ:
            halo[128:256] = x[b, q0 + 2048:q0 + 2176]
            ropeB_h[128:256] = ropeB[q0 + 2048:q0 + 2176]
        m = dict(shared)
        m.update({
            "x_own": f(x[b, own]), "x_oth": f(x[b, oth_idx]), "x_halo": halo,
            "c_pk": pk(c[b]),
            "ropeA_own": f(ropeA[own]), "ropeA_oth": f(ropeA[oth_idx]),
            "ropeB_own": f(ropeB[own]), "ropeB_halo": ropeB_h,
            "masks": _masks(core),
        })
        in_maps.append(m)
    return in_maps


_NC_CACHE = {}


def kernel(x, c, w_ada, b_ada, norm1_g, w_in, q_norm_a, k_norm_a, sink_b,
           w_branch, w_out, norm2_g, w_mlp_in, w_mlp_out, final_g):
    in_maps = make_in_maps(x, c, w_ada, b_ada, norm1_g, w_in, q_norm_a, k_norm_a, sink_b,
                           w_branch, w_out, norm2_g, w_mlp_in, w_mlp_out, final_g)
    if "nc" not in _NC_CACHE:
        _NC_CACHE["nc"] = build()
    nc = _NC_CACHE["nc"]
    res = run_bass_kernel_spmd(nc, in_maps, core_ids=list(range(N_CORES)))
    outp = np.zeros((2, SEQ, D), np.float32)
    for core in range(N_CORES):
        b, qi = divmod(core, 4)
        outp[b, qi * 2048:(qi + 1) * 2048] = np.asarray(res.results[core]["out"], dtype=np.float32)
    return outp
```

```python
from contextlib import ExitStack
import os
import numpy as np
import concourse.bass as bass
import concourse.mybir as mybir
from concourse.bass_utils import run_bass_kernel_spmd

F32 = mybir.dt.float32
BF16 = mybir.dt.bfloat16
U8 = mybir.dt.uint8
ALU = mybir.AluOpType
AF = mybir.ActivationFunctionType
AX = mybir.AxisListType

N_CORES = 8
D = 1024
SEQ = 8192
NOWN = 16
NOTH = 48
EPS = 1e-6
NEG = -30000.0


class _Op:
    __slots__ = ("eng", "fn", "dma", "deps", "ticket", "has_dep", "idx", "dma_total")


class Sched:
    ENGS = ("pe", "act", "dve", "pool", "sp")

    def __init__(self, nc):
        self.nc = nc
        self.ops = []
        self.last_w = {}
        self.readers = {}
        self.dma_count = {}
        self.region_last = {}
        self.alias = {}

    def add(self, eng, fn, reads=(), writes=(), dma=None):
        op = _Op()
        op.eng, op.fn, op.dma = eng, fn, dma
        op.idx = len(self.ops)
        op.has_dep = False
        op.ticket = None
        op.dma_total = None
        cand = {}
        ps_r = [k for k in reads if k[0] == "ps"]
        if ps_r:
            reads = [k for k in reads if k[0] != "ps"]
            writes = list(writes) + [k for k in ps_r if k not in writes]
        for k in reads:
            w = self.last_w.get(k)
            if w is not None:
                cand[w] = "raw"
        for k in writes:
            w = self.last_w.get(k)
            if w is not None:
                cand.setdefault(w, "waw")
            for r in self.readers.get(k, {}).values():
                cand.setdefault(r, "war")
        regions = set(k[0] for k in reads) | set(k[0] for k in writes)
        for r in regions:
            for old in self.alias.get(r, ()):
                for idx in self.region_last.get(old, {}).values():
                    cand.setdefault(idx, "alias")
        best = {}
        for p, kind in cand.items():
            pop = self.ops[p]
            if pop.dma is None and dma is None and pop.eng == eng:
                if eng == "pe":
                    continue
            key = ("d", pop.dma) if pop.dma is not None else ("e", pop.eng)
            if p > best.get(key, -1):
                best[key] = p
        op.deps = sorted(best.values())
        for p in op.deps:
            self.ops[p].has_dep = True
        mykey = ("d", dma) if dma is not None else ("e", eng)
        for k in writes:
            self.last_w[k] = op.idx
            self.readers[k] = {}
        for k in reads:
            if k not in writes:
                self.readers.setdefault(k, {})[mykey] = op.idx
        for r in regions:
            self.region_last.setdefault(r, {})[mykey] = op.idx
        if dma is not None:
            self.dma_count[dma] = self.dma_count.get(dma, 0) + 1
            op.dma_total = 16 * self.dma_count[dma]
        self.ops.append(op)
        return op

    def emit(self):
        nc = self.nc
        cnt = {e: 0 for e in self.ENGS}
        for op in self.ops:
            if op.dma is None and op.has_dep:
                cnt[op.eng] += 1
                op.ticket = cnt[op.eng]
        with ExitStack() as es:
            esem = {e: es.enter_context(nc.semaphore("s_" + e)) for e in self.ENGS}
            dsem = {d: es.enter_context(nc.semaphore("d_" + str(d))) for d in self.dma_count}
            block = es.enter_context(nc.Block())
            ops = self.ops

            def make(engname):
                def body(eng):
                    waited = {}
                    for op in ops:
                        if op.eng != engname:
                            continue
                        need = {}
                        for p in op.deps:
                            pop = ops[p]
                            if pop.dma is not None:
                                key, val = ("d", pop.dma), pop.dma_total
                            else:
                                key, val = ("e", pop.eng), pop.ticket
                            if val > need.get(key, 0):
                                need[key] = val
                        for key, val in need.items():
                            if waited.get(key, 0) >= val:
                                continue
                            waited[key] = val
                            sem = dsem[key[1]] if key[0] == "d" else esem[key[1]]
                            eng.wait_ge(sem, val)
                        ins = op.fn(eng)
                        if op.dma is not None:
                            ins.then_inc(dsem[op.dma], 16)
                        elif op.has_dep:
                            ins.then_inc(esem[engname], 1)
                return body

            block.tensor(make("pe"))
            block.scalar(make("act"))
            block.vector(make("dve"))
            block.gpsimd(make("pool"))
            block.sync(make("sp"))


class Arena:
    def __init__(self, S, tensor, total):
        self.S, self.t, self.total = S, tensor, total
        self.live = {}
        self.dead = []
        self.peak = 0
        self.flat = {}

    def alloc(self, name, nbytes, at=None):
        nbytes = (nbytes + 63) // 64 * 64
        pos = 0
        if at is not None:
            pos = at
            for n_, (o, s) in self.live.items():
                assert not (o < pos + nbytes and pos < o + s), f"{name}@{at} overlaps live {n_} {(o, s)}"
        else:
            for (o, s) in sorted(self.live.values()):
                if o - pos >= nbytes:
                    break
                pos = max(pos, o + s)
        if pos + nbytes > self.total:
            raise RuntimeError(f"arena overflow allocating {name} {nbytes} at {pos}; live={self.live}")
        assert name not in self.live and name not in self.S.alias
        self.live[name] = (pos, nbytes)
        self.peak = max(self.peak, pos + nbytes)
        self.S.alias[name] = [n for (o, s, n) in self.dead if o < pos + nbytes and pos < o + s]
        return pos

    def free(self, *names):
        for name in names:
            o, s = self.live.pop(name)
            self.dead.append((o, s, name))

    def top(self):
        return max(o + s for (o, s) in self.live.values())

    def tile(self, name, free_shape, dt, at=None):
        n = int(np.prod(free_shape))
        sz = 4 if dt == F32 else 2
        off = self.alloc(name, n * sz, at)
        v = self.t[:, off:off + n * sz].bitcast(dt)
        self.flat[name] = (v, dt)
        if len(free_shape) == 2:
            v = v.rearrange("p (a b) -> p a b", a=free_shape[0])
        elif len(free_shape) == 3:
            v = v.rearrange("p (a b c) -> p a b c", a=free_shape[0], b=free_shape[1])
        return v


def build(stop_after=99, dumps=(), p1=(NOWN, 2, NOTH), p1parts=15):
    nc = bass.Bass("TRN2", target_bir_lowering=False)

    def din(name, shape):
        return nc.dram_tensor(name, list(shape), F32, kind="ExternalInput").ap()

    x_own = din("x_own", [NOWN * 128, D])
    x_oth = din("x_oth", [NOTH * 128, D])
    x_halo = din("x_halo", [256, D])
    c_pk = din("c_pk", [128, 8])
    w_ada = din("w_ada", [D, 6 * D])
    bada_pk = din("bada_pk", [128, 48])
    g1_pk = din("g1_pk", [128, 8])
    g2_pk = din("g2_pk", [128, 8])
    gf = din("gf", [D])
    w_in = din("w_in", [D, 3584])
    qn_g = din("qn_g", [64])
    kn_g = din("kn_g", [64])
    sink = din("sink", [1, 8])
    w_br = din("w_br", [2, 512, D])
    w_out = din("w_out", [D, D])
    w1 = din("w1", [D, 4 * D])
    w2 = din("w2", [4 * D, D])
    ropeA_own = din("ropeA_own", [NOWN * 128, 128])
    ropeA_oth = din("ropeA_oth", [NOTH * 128, 128])
    ropeB_own = din("ropeB_own", [NOWN * 128, 128])
    ropeB_halo = din("ropeB_halo", [256, 128])
    masks = din("masks", [4, 128, 512])
    out = nc.dram_tensor("out", [NOWN * 128, D], F32, kind="ExternalOutput").ap()

    ARENA_BYTES = 212736
    with ExitStack() as es:
        arena_t = es.enter_context(nc.sbuf_tensor("arena", [128, ARENA_BYTES], U8))
        psum_t = es.enter_context(nc.psum_tensor("psum", [128, 16384], U8))
        S = Sched(nc)
        A = Arena(S, arena_t, ARENA_BYTES)

        def PS(b0, nb=1, dt=F32):
            return psum_t[:, b0 * 2048:(b0 + nb) * 2048].bitcast(dt)

        def psk(b0, nb=1):
            return [("ps", b) for b in range(b0, b0 + nb)]

        def mm(o, lhsT, rhs, start, stop, reads, writes):
            S.add("pe", lambda e: e.matmul(o, lhsT=lhsT, rhs=rhs, start=start, stop=stop), reads, writes)

        def tr(o, i, ident, reads, writes):
            S.add("pe", lambda e: e.transpose(out=o, in_=i, identity=ident), reads, writes)

        def act(o, i, func, reads, writes, **kw):
            S.add("act", lambda e: e.activation(out=o, in_=i, func=func, **kw), reads, writes)

        def tt(eng, o, i0, i1, op, reads, writes):
            S.add(eng, lambda e: e.tensor_tensor(out=o, in0=i0, in1=i1, op=op), reads, writes)

        def ts(eng, o, i0, s1, s2, op0, op1, reads, writes):
            S.add(eng, lambda e: e.tensor_scalar(out=o, in0=i0, scalar1=s1, scalar2=s2, op0=op0, op1=op1), reads, writes)

        def stt(eng, o, i0, sc, i1, op0, op1, reads, writes):
            S.add(eng, lambda e: e.scalar_tensor_tensor(out=o, in0=i0, scalar=sc, in1=i1, op0=op0, op1=op1), reads, writes)

        def cp(eng, o, i, reads, writes):
            S.add(eng, lambda e: e.tensor_copy(out=o, in_=i), reads, writes)

        def recip(o, i, reads, writes):
            S.add("dve", lambda e: e.reciprocal(out=o, in_=i), reads, writes)

        def memset(eng, o, val, writes):
            S.add(eng, lambda e: e.memset(o, val), (), writes)

        def dma(eng, o, i, sem, reads, writes):
            S.add(eng, lambda e: e.dma_start(out=o, in_=i), reads, writes, dma=sem)

        ident_f = A.tile("ident_f", [128], F32)
        ident_b = A.tile("ident_b", [128], BF16)
        ones_f = A.tile("ones_f", [64], F32)
        modT = A.tile("modT", [48], F32)
        G1c = A.tile("G1c", [8], F32)
        G2c = A.tile("G2c", [8], F32)
        gate1_bc = A.tile("gate1_bc", [D], F32)
        gate2_bc = A.tile("gate2_bc", [D], F32)
        gf_bc = A.tile("gf_bc", [D], F32)
        rstd_own = A.tile("rstd_own", [NOWN], F32)
        stat = A.tile("stat", [128], F32)
        Y_OFF = A.top()
        wa = A.tile("wa", [2, 8, 512], F32)
        X1_OFF = A.top()
        qn_bc = A.tile("qn_bc", [64], F32)
        kn_bc = A.tile("kn_bc", [64], F32)
        sel = A.tile("sel", [192], BF16)
        knsw_bc = A.tile("knsw_bc", [64], F32)
        qnsw_bc = A.tile("qnsw_bc", [64], F32)
        sc = A.tile("sc", [8], F32)
        csb = A.tile("csb", [8], F32)
        bada = A.tile("bada", [48], F32)
        g1c = A.tile("g1c", [8], F32)
        g2c = A.tile("g2c", [8], F32)
        sinkt = A.tile("sinkt", [8], F32)
        esf = A.tile("esf", [8], F32)
        gcolb = A.tile("gcolb", [2, 128], F32)

        memset("pool", ident_f, 0.0, [("ident_f",)])
        S.add("pool", lambda e: e.affine_select(out=ident_f, in_=ident_f, pattern=[[-1, 128]],
                                                compare_op=ALU.not_equal, fill=1.0, base=0,
                                                channel_multiplier=1),
              [("ident_f",)], [("ident_f",)])
        cp("dve", ident_b, ident_f, [("ident_f",)], [("ident_b",)])
        memset("pool", ones_f, 1.0, [("ones_f",)])
        memset("pool", sel, 0.0, [("sel",)])
        memset("pool", sel[:, 64:128], 1.0, [("sel",)])

        dma("sp", csb, c_pk, "c_c", (), [("csb",)])
        dma("sp", bada, bada_pk, "c_bada", (), [("bada",)])
        dma("sp", g1c, g1_pk, "c_g1", (), [("g1c",)])
        act(sc, csb, AF.Silu, [("csb",)], [("sc",)])

        w_ada_v = w_ada.rearrange("(kc p) n -> p kc n", p=128)
        MODBANK = 7
        modps = PS(MODBANK)[:, 320:368]

        def mod_dma(blk):
            sl = blk % 2
            dma("sp", wa[:, sl], w_ada_v[:, :, blk * 512:(blk + 1) * 512], f"wa{sl}", (), [("wa", sl)])

        def mod_block(blk):
            sl = blk % 2
            for jj in range(4):
                j = blk * 4 + jj
                for kc in range(8):
                    mm(modps[:, j:j + 1], wa[:, sl, kc, jj * 128:(jj + 1) * 128], sc[:, kc:kc + 1],
                       kc == 0, kc == 7, [("wa", sl), ("sc",)], psk(MODBANK))

        mod_dma(0)
        mod_dma(1)
        for blk in range(4):
            mod_block(blk)
            mod_dma(blk + 2)
        tt("dve", modT[:, 0:16], modps[:, 0:16], bada[:, 0:16], ALU.add, psk(MODBANK) + [("bada",)], [("modT", "a")])
        S1c = modT[:, 0:8]
        S2c = modT[:, 24:32]
        stt("dve", G1c, modT[:, 8:16], 1.0, g1c, ALU.add, ALU.mult, [("modT", "a"), ("g1c",)], [("G1c",)])

        dma("sp", g2c, g2_pk, "c_g2", (), [("g2c",)])
        dma("sp", sinkt[0:1, :], sink, "c_sink", (), [("sinkt",)])
        dma("sp", gf_bc, gf.partition_broadcast(128), "c_gf", (), [("gf_bc",)])
        dma("sp", qn_bc, qn_g.partition_broadcast(128), "c_qn", (), [("qn_bc",)])
        dma("sp", kn_bc, kn_g.partition_broadcast(128), "c_kn", (), [("kn_bc",)])
        for (src_, dst_, sk, dk) in ((kn_bc, knsw_bc, "kn_bc", "knsw_bc"), (qn_bc, qnsw_bc, "qn_bc", "qnsw_bc")):
            knv = src_.rearrange("p (b t i) -> p b t i", b=2, t=2)
            ksv = dst_.rearrange("p (b t i) -> p b t i", b=2, t=2)
            for t_ in range(2):
                cp("pool", ksv[:, :, t_, :], knv[:, :, 1 - t_, :], [(sk,)], [(dk, t_)])

        KT_B = A.tile("KT_B", [18 * 128], BF16)
        V_B = A.tile("V_B", [18, 192], BF16)
        QT_B = A.tile("QT_B", [4, NOWN * 128], BF16)
        KT_A = A.tile("KT_A", [SEQ], BF16)
        V_A = A.tile("V_A", [64, 192], BF16)
        QT_A = A.tile("QT_A", [4, NOWN * 128], BF16)
        A_END = A.top()
        assert A_END - 16 * 1024 >= X1_OFF + 64 * 1024
        memset("pool", V_A[:, :, 64:128], 1.0, [("V_A",)])
        memset("pool", V_B[:, :, 64:128], 1.0, [("V_B",)])

        Wb = A.tile("Wb", [8, 1536], BF16)
        dma("pool", Wb, w_in.rearrange("(kc p) n -> p kc n", p=128)[:, :, 0:1536], "w_in", (), [("Wb",)])
        xs = A.tile("xs", [2, D], F32)
        rp = A.tile("rp", [3, 256], F32)
        rpg = A.tile("rpg", [3, 4, 64], F32)
        xn = A.tile("xn", [1, D], BF16)
        hT = A.tile("hT", [2, 8, 128], BF16)
        wkq = A.tile("wkq", [2, 3, 512], F32)
        wkk = A.tile("wkk", [3, 3, 128], F32)
        qrq = A.tile("qrq", [6, 512], BF16)
        qrk = A.tile("qrk", [6, 128], BF16)

        tiles = ([("own", t) for t in range(p1[0])] + [("halo", h) for h in range(p1[1])]
                 + [("oth", u) for u in range(p1[2])])
        NT = len(tiles)

        def stageA(ti):
            kind, idx = tiles[ti]
            s3, s2 = ti % 3, ti % 2
            sx = ti % 2
            if kind == "own":
                xa = x_own[idx * 128:(idx + 1) * 128, :]
                ropes = [ropeA_own[idx * 128:(idx + 1) * 128, :], ropeB_own[idx * 128:(idx + 1) * 128, :]]
            elif kind == "oth":
                xa = x_oth[idx * 128:(idx + 1) * 128, :]
                ropes = [ropeA_oth[idx * 128:(idx + 1) * 128, :]]
            else:
                xa = x_halo[idx * 128:(idx + 1) * 128, :]
                ropes = [ropeB_halo[idx * 128:(idx + 1) * 128, :]]
            dma("sp", xs[:, sx], xa, f"xs{sx}", (), [("xs", sx)])
            off = 0
            for rap in ropes:
                dma("sp", rp[:, s3, off:off + 128], rap, f"rp{s3}_{off}", (), [("rp", s3, off)])
                off += 128
            ssc = stat[:, s2:s2 + 1]
            act(xn[:, 0], xs[:, sx], AF.Square, [("xs", sx)], [("xn",), ("stat", "ss", s2)], accum_out=ssc)
            rsc = stat[:, 2 + s2:3 + s2]
            act(rsc, ssc, AF.Ln, [("stat", "ss", s2)], [("stat", "rs", s2)], scale=1.0 / D, bias=EPS)
            if kind == "own":
                rstd, rkey = rstd_own[:, idx:idx + 1], ("rstd_own", idx)
            else:
                rstd, rkey = stat[:, 4 + s2:5 + s2], ("stat", "rstd", s2)
            act(rstd, rsc, AF.Exp, [("stat", "rs", s2)], [rkey], scale=-0.5)
            act(xn[:, 0], xs[:, sx], AF.Copy, [("xs", sx), rkey], [("xn",)], scale=rstd)
            if kind != "halo":
                tt("pool", rpg[:, s3, 0, :], rp[:, s3, 0:64], kn_bc, ALU.mult, [("rp", s3, 0), ("kn_bc",)],
                   [("rpg", s3, 0)])
                tt("pool", rpg[:, s3, 1, :], rp[:, s3, 64:128], knsw_bc, ALU.mult,
                   [("rp", s3, 0), ("knsw_bc", 0), ("knsw_bc", 1)], [("rpg", s3, 1)])
            if kind == "own":
                tt("pool", rpg[:, s3, 2, :], rp[:, s3, 0:64], qn_bc, ALU.mult, [("rp", s3, 0), ("qn_bc",)],
                   [("rpg", s3, 2)])
                tt("pool", rpg[:, s3, 3, :], rp[:, s3, 64:128], qnsw_bc, ALU.mult,
                   [("rp", s3, 0), ("qnsw_bc", 0), ("qnsw_bc", 1)], [("rpg", s3, 3)])

        def hbank(kc):
            return (0, kc * 128) if kc < 2 else (1, (kc - 2) * 128)

        def stageB(ti):
            s2 = ti % 2
            for kc in range(8):
                bk, c0 = hbank(kc)
                tr(PS(bk, 1, BF16)[:, c0:c0 + 128], xn[:, 0, kc * 128:(kc + 1) * 128], ident_b,
                   [("xn",), ("ident_b",)], psk(bk))
            for kc in range(8):
                bk, c0 = hbank(kc)
                o = hT[:, s2, kc, :]
                i = PS(bk, 1, BF16)[:, c0:c0 + 128]
                if bk == 0:
                    act(o, i, AF.Identity, psk(bk) + [("G1c",), ("modT", "a")], [("hT", s2, kc)],
                        scale=G1c[:, kc:kc + 1], bias=S1c[:, kc:kc + 1])
                else:
                    ts("dve", o, i, G1c[:, kc:kc + 1], S1c[:, kc:kc + 1], ALU.mult, ALU.add,
                       psk(bk) + [("G1c",), ("modT", "a")], [("hT", s2, kc)])

        def proj(s2, c0, c1, bank):
            for kc in range(8):
                mm(PS(bank)[:, 0:c1 - c0], hT[:, s2, kc, :], Wb[:, kc, c0:c1], kc == 0, kc == 7,
                   [("hT", s2, kc), ("Wb",)], psk(bank))

        def chain(ps_view, bank, H, norm, gains, tabC, tabS, tab_keys, nb, wf, wkey, scol, qr_v, qr_key,
                  trbank, trcol, final):
            n = H * 64
            hw = 32 // nb
            v3 = lambda a: a.rearrange("p (h d) -> p h d", h=H)
            f0, f1, f2 = wf[0][:, 0:n], wf[1][:, 0:n], wf[2][:, 0:n]
            k0, k1, k2 = wkey + (0,), wkey + (1,), wkey + (2,)
            act(f0, ps_view, AF.Copy, psk(bank), [k0])
            yield
            tt("pool", v3(f2), v3(f0), tabC.unsqueeze(1).to_broadcast([128, H, 64]), ALU.mult,
               [k0] + tab_keys, [k2])
            if norm:
                ssh = stat[:, scol:scol + H]
                rh = stat[:, scol + 8:scol + 8 + H]
                for h_ in range(H):
                    act(f1[:, h_ * 64:(h_ + 1) * 64], ps_view[:, h_ * 64:(h_ + 1) * 64], AF.Square, psk(bank),
                        [k1, ("stat", "ssh", scol)], accum_out=ssh[:, h_:h_ + 1])
            yield
            if norm:
                act(rh, ssh, AF.Ln, [("stat", "ssh", scol)], [("stat", "rh", scol)], scale=1.0 / 64, bias=EPS)
                act(rh, rh, AF.Exp, [("stat", "rh", scol)], [("stat", "rh", scol)], scale=-0.5)
            sv = f0.rearrange("p (h b t i) -> p h b t i", h=H, b=nb, t=2)
            bv = f1.rearrange("p (h b t i) -> p h b t i", h=H, b=nb, t=2)
            Sv = tabS.rearrange("p (b t i) -> p b t i", b=nb, t=2)
            for t_ in range(2):
                tt("pool", bv[:, :, :, t_, :], sv[:, :, :, 1 - t_, :],
                   Sv[:, :, t_, :].unsqueeze(1).to_broadcast([128, H, nb, hw]), ALU.mult,
                   [k0] + tab_keys, [k1])
            yield
            if H == 8:
                p4 = lambda a: a.rearrange("p (g j d) -> p g j d", g=2, j=4)
                va, vb = p4(f2), p4(f1)
                ov = qr_v.rearrange("p (j g d) -> p g j d", j=4, g=2)
            else:
                va, vb, ov = v3(f2), v3(f1), v3(qr_v)
            if norm:
                tt("dve", va, va, vb, ALU.add, [k1, k2], [k2])
                yield
                if H == 8:
                    rb = rh.rearrange("p (g j) -> p g j", g=2).unsqueeze(3).to_broadcast([128, 2, 4, 64])
                else:
                    rb = rh.unsqueeze(2).to_broadcast([128, H, 64])
                tt("dve", ov, va, rb, ALU.mult, [k2, ("stat", "rh", scol)], [qr_key])
            else:
                tt("dve", ov, va, vb, ALU.add, [k1, k2], [qr_key])
            yield

        def chain_tail(H, qr_v, qr_key, trbank, trcol, final):
            n = H * 64
            pTv = PS(trbank, 1, BF16)
            for j in range(n // 128):
                tr(pTv[:, trcol + j * 128:trcol + (j + 1) * 128], qr_v[:, j * 128:(j + 1) * 128], ident_b,
                   [qr_key, ("ident_b",)], psk(trbank))
            dst, dkey = final
            if n == 512:
                cp("dve", dst, pTv[:, trcol:trcol + 512].rearrange("p (j q) -> p j q", j=4), psk(trbank), [dkey])
            else:
                cp("dve", dst, pTv[:, trcol:trcol + 128], psk(trbank), [dkey])

        def run_chains(gens):
            gens = list(gens)
            while gens:
                for g_ in list(gens):
                    try:
                        next(g_)
                    except StopIteration:
                        gens.remove(g_)

        def vcopy(dst3, bank, vkey):
            act(dst3.rearrange("p (a d) -> p a d", a=3)[:, 0:3:2, :],
                PS(bank)[:, 128:256].rearrange("p (a d) -> p a d", a=2), AF.Copy, psk(bank), [vkey])

        def stageC(ti):
            kind, idx = tiles[ti]
            s3, s2 = ti % 3, ti % 2
            rA = rp[:, s3, 0:128]
            rB = rp[:, s3, 128:256] if kind == "own" else rp[:, s3, 0:128]
            kA = [("rp", s3, 0)]
            kB = [("rp", s3, 128)] if kind == "own" else [("rp", s3, 0)]
            kG = [("rpg", s3, 0), ("rpg", s3, 1)]
            gens = []
            tails = []
            par = ti % 3

            def add(ps_view, bank, H, norm, tabC, tabS, tkeys, nb, wslot, wk_t, wname, scol, qr_t, qname, qslot,
                    trbank, trcol, final):
                wf = [wk_t[:, wslot, f, :] for f in range(3)]
                qv = qr_t[:, qslot, :]
                gens.append(chain(ps_view, bank, H, norm, None, tabC, tabS, tkeys, nb, wf, (wname, wslot), scol,
                                  qv, (qname, qslot), trbank, trcol, final))
                tails.append(lambda: chain_tail(H, qv, (qname, qslot), trbank, trcol, final))

            if kind == "own":
                t = idx
                proj(s2, 0, 512, 2)
                proj(s2, 512, 768, 3)
                proj(s2, 768, 1280, 4)
                proj(s2, 1280, 1536, 5)
                vcopy(V_A[:, t, :], 3, ("V_A",))
                vcopy(V_B[:, t + 1, :], 5, ("V_B",))
                kQ = [("rpg", s3, 2), ("rpg", s3, 3)]
                add(PS(2), 2, 8, True, rpg[:, s3, 2, :], rpg[:, s3, 3, :], kQ, 2, 0, wkq, "wkq", 8,
                    qrq, "qrq", 0 + par, 6, 0, (QT_A[:, :, t * 128:(t + 1) * 128], ("QT_A",)))
                add(PS(3)[:, 0:128], 3, 2, True, rpg[:, s3, 0, :], rpg[:, s3, 1, :], kG, 2, 0, wkk, "wkk", 24,
                    qrk, "qrk", 0 + par, 6, 512, (KT_A[:, t * 128:(t + 1) * 128], ("KT_A",)))
                add(PS(4), 4, 8, False, rB[:, 0:64], rB[:, 64:128], kB, 1, 1, wkq, "wkq", 0,
                    qrq, "qrq", 3 + par, 7, 0, (QT_B[:, :, t * 128:(t + 1) * 128], ("QT_B",)))
                add(PS(5)[:, 0:128], 5, 2, False, rB[:, 0:64], rB[:, 64:128], kB, 1, 1, wkk, "wkk", 0,
                    qrk, "qrk", 3 + par, 7, 512, (KT_B[:, (t + 1) * 128:(t + 2) * 128], ("KT_B",)))
            elif kind == "oth":
                kt = NOWN + idx
                bank = 2 + ti % 4
                w_ = (0, 2)[ti % 2]
                proj(s2, 512, 768, bank)
                vcopy(V_A[:, kt, :], bank, ("V_A",))
                add(PS(bank)[:, 0:128], bank, 2, True, rpg[:, s3, 0, :], rpg[:, s3, 1, :], kG, 2, w_, wkk, "wkk",
                    24 + 16 * (ti % 2), qrk, "qrk", 0 + par, 6 + ti % 2, 0,
                    (KT_A[:, kt * 128:(kt + 1) * 128], ("KT_A",)))
            else:
                kb = 0 if idx == 0 else 17
                proj(s2, 1280, 1536, 5)
                vcopy(V_B[:, kb, :], 5, ("V_B",))
                add(PS(5)[:, 0:128], 5, 2, False, rB[:, 0:64], rB[:, 64:128], kB, 1, 1, wkk, "wkk", 0,
                    qrk, "qrk", 3 + par, 7, 512, (KT_B[:, kb * 128:(kb + 1) * 128], ("KT_B",)))
            run_chains(gens)
            return tails

        MOD_EVERY = 2
        if stop_after >= 1:
            next_blk = 4
            tailq = [[], []]
            for step in range(NT + 4):
                for tl in tailq.pop(0):
                    tl()
                if 1 <= step <= NT:
                    stageB(step - 1)
                if step < NT:
                    stageA(step)
                new_tails = []
                if 2 <= step <= NT + 1:
                    new_tails = stageC(step - 2)
                tailq.append(new_tails)
                if step >= 2 and step % MOD_EVERY == 0 and next_blk < 12:
                    mod_block(next_blk)
                    if next_blk + 2 < 12:
                        mod_dma(next_blk + 2)
                    next_blk += 1
            while next_blk < 12:
                mod_block(next_blk)
                if next_blk + 2 < 12:
                    mod_dma(next_blk + 2)
                next_blk += 1
        else:
            for blk in range(4, 12):
                mod_block(blk)
                if blk + 2 < 12:
                    mod_dma(blk + 2)
        tt("dve", modT[:, 16:48], modps[:, 16:48], bada[:, 16:48], ALU.add, psk(MODBANK) + [("bada",)], [("modT", "b")])
        stt("dve", G2c, modT[:, 32:40], 1.0, g2c, ALU.add, ALU.mult, [("modT", "b"), ("g2c",)], [("G2c",)])
        act(esf[0:1, :], sinkt[0:1, :], AF.Exp, [("sinkt",)], [("esf",)])
        for gi, (gbc, base, bank) in enumerate(((gate1_bc, 16, 1), (gate2_bc, 40, 3))):
            gps = PS(bank, 2)
            for cc in range(8):
                sl = cc % 2
                cp("dve", gcolb[:, sl], modT[:, base + cc:base + cc + 1].to_broadcast([128, 128]),
                   [("modT", "b")], [("gcolb", sl)])
                mm(gps[:, cc * 128:(cc + 1) * 128], gcolb[:, sl], ident_f, True, True,
                   [("gcolb", sl), ("ident_f",)], psk(bank + cc // 4))
            cp("dve", gbc, gps, psk(bank, 2), [("gate_bc", gi)])
        A.free("Wb", "xs", "rp", "rpg", "xn", "hT", "wkq", "wkk", "qrq", "qrk", "wa",
               "sc", "csb", "bada", "g1c", "g2c", "sinkt", "gcolb", "knsw_bc", "qnsw_bc")
        yaT = A.tile("yaT", [4, NOWN * 128], BF16, at=Y_OFF)
        ybT = A.tile("ybT", [4, NOWN * 128], BF16, at=Y_OFF + 16 * 1024)
        maskb = A.tile("maskb", [4, 512], BF16, at=ARENA_BYTES - 4096)
        esink = A.tile("esink", [8, 128], BF16, at=ARENA_BYTES - 4096 - 2048)
        dma("pool", maskb, masks.rearrange("m p n -> p m n"), "c_mask", (), [("maskb",)])
        cp("dve", esink[0:1, :, :], esf[0:1, :].unsqueeze(2).to_broadcast([1, 8, 128]), [("esf",)], [("esink",)])
        A.free("esf")
        Wg = A.tile("Wg", [8, 2048], BF16)
        if stop_after >= 4:
            dma("pool", Wg, w_in.rearrange("(kc p) n -> p kc n", p=128)[:, :, 1536:3584], "w_g", (), [("Wg",)])

        SCALE = 0.125
        PT = A.tile("PT", [3, 1024], BF16)
        osb = A.tile("osb", [2, 2, 512], F32)
        rrow = A.tile("rrow", [2, 512], F32)
        P2_END = max(A.live[n_][0] + A.live[n_][1] for n_ in ("Wg", "PT", "osb", "rrow"))

        def norm_part1(qt, par, obanks, gsel):
            for g in gsel:
                r = 64 if g == 0 else 0
                cp("dve", osb[:, par, g, :], PS(obanks[g]), psk(obanks[g]), [("osb", par, g)])
                recip(rrow[r:r + 1, par, :], osb[r:r + 1, par, g, :],
                      [("osb", par, g)], [("rrow", par, g)])

        def norm_part2(qt, par, bcbanks, gsel, dstT, dkey):
            for g in gsel:
                r = 64 if g == 0 else 0
                bcbank = bcbanks[g]
                mm(PS(bcbank)[g * 64:(g + 1) * 64, :], ones_f[r:r + 1, 0:64],
                   rrow[r:r + 1, par, :], True, True,
                   [("rrow", par, g), ("ones_f",)], psk(bcbank))
                tt("dve", dstT[g * 64:(g + 1) * 64, :, qt * 128:(qt + 1) * 128],
                   osb[g * 64:(g + 1) * 64, par, g, :].rearrange("p (j q) -> p j q", j=4),
                   PS(bcbank)[g * 64:(g + 1) * 64, :].rearrange("p (j q) -> p j q", j=4), ALU.mult,
                   [("osb", par, g)] + psk(bcbank), [dkey])

        if stop_after >= 2:
            NKT = 64
            AHEAD = 2

            def qk2(it):
                qt, kt = divmod(it, NKT)
                sb_ = it % 3
                for g in range(2):
                    mm(PS(2 * sb_ + g), KT_A[g * 64:(g + 1) * 64, kt * 128:(kt + 1) * 128],
                       QT_A[g * 64:(g + 1) * 64, :, qt * 128:(qt + 1) * 128], True, True,
                       [("KT_A",), ("QT_A",)], psk(2 * sb_ + g))

            deferred = {}
            total = NOWN * NKT
            for it in range(min(AHEAD, total)):
                qk2(it)
            for it in range(total):
                qt, kt = divmod(it, NKT)
                sb_, pb = it % 3, it % 3
                act(PT[:, pb, :], PS(2 * sb_, 2), AF.Exp, psk(2 * sb_, 2), [("PT", pb)], scale=SCALE)
                if it + AHEAD < total:
                    qk2(it + AHEAD)
                for g in range(2):
                    mm(PS(6 + g), V_A[:, kt, g * 64:g * 64 + 128], PT[:, pb, g * 512:(g + 1) * 512],
                       kt == 0, kt == NKT - 1, [("PT", pb), ("V_A",)], psk(6 + g))
                if kt == NKT - 1:
                    par = qt % 2
                    norm_part1(qt, par, (6, 7), (0, 1))
                    deferred[it + 2] = (lambda qt=qt, par=par: norm_part2(qt, par, (0, 2), (0, 1), yaT, ("yaT",)))
                if it in deferred:
                    deferred.pop(it)()
            for k in sorted(deferred):
                deferred[k]()
        A.free("PT", "KT_A", "V_A", "QT_A")
        Wo = A.tile("Wo", [8, D], BF16, at=A_END - 16 * 1024)
        WbrA = A.tile("WbrA", [4, D], BF16, at=P2_END + 8 * 1024)
        WbrB = A.tile("WbrB", [4, D], BF16, at=P2_END)
        if stop_after >= 4:
            for bi, (wt, nm) in enumerate(((WbrA, "WbrA"), (WbrB, "WbrB"))):
                src = w_br[bi].rearrange("(g j d) n -> g d j n", g=2, j=4)
                for g in range(2):
                    dma("pool", wt[g * 64:(g + 1) * 64, :, :], src[g], "w_" + nm, (), [(nm, g)])
            dma("pool", Wo, w_out.rearrange("(kc p) n -> p kc n", p=128), "w_o", (), [("Wo",)])

        PTB = A.tile("PTB", [2, 1536], BF16)
        if stop_after >= 3:
            def qk3(i):
                qt, g = divmod(i, 2)
                sb_ = i % 2
                for blk in range(3):
                    kb = qt + blk
                    bank = 3 * sb_ + blk
                    mm(PS(bank), KT_B[g * 64:(g + 1) * 64, kb * 128:(kb + 1) * 128],
                       QT_B[g * 64:(g + 1) * 64, :, qt * 128:(qt + 1) * 128], True, blk == 1,
                       [("KT_B",), ("QT_B",)], psk(bank))
                for blk in (0, 2):
                    bank = 3 * sb_ + blk
                    mi = (0 if qt == 0 else 1) if blk == 0 else (3 if qt == NOWN - 1 else 2)
                    mm(PS(bank), ident_b, maskb[:, mi, :], False, True,
                       [("ident_b",), ("maskb",)], psk(bank))

            total = NOWN * 2
            deferred = {}
            qk3(0)
            for i in range(total):
                qt, g = divmod(i, 2)
                sb_ = i % 2
                act(PTB[:, sb_, :], PS(3 * sb_, 3), AF.Exp, psk(3 * sb_, 3), [("PTB", sb_)], scale=SCALE)
                if i + 1 < total:
                    qk3(i + 1)
                for blk in range(3):
                    kb = qt + blk
                    mm(PS(6), V_B[:, kb, g * 64:g * 64 + 128], PTB[:, sb_, blk * 512:(blk + 1) * 512],
                       blk == 0, False, [("PTB", sb_), ("V_B",)], psk(6))
                mm(PS(6), sel[0:1, (0 if g == 0 else 64):(128 if g == 0 else 192)],
                   esink[0:1, g * 4:(g + 1) * 4, :].rearrange("p j q -> p (j q)"), False, True,
                   [("sel",), ("esink",)], psk(6))
                par = i % 2
                norm_part1(qt, par, (6, 6), (g,))
                deferred[i + 1] = (lambda qt=qt, par=par, g=g: norm_part2(qt, par, (7, 7), (g,), ybT, ("ybT",)))
                if i in deferred:
                    deferred.pop(i)()
            for k in sorted(deferred):
                deferred[k]()
        A.free("PTB", "osb", "rrow", "KT_B", "V_B", "QT_B",
               "qn_bc", "kn_bc", "maskb", "sel", "esink")

        x1 = A.tile("x1", [NOWN, D], F32, at=X1_OFF)
        xs = A.tile("xs4", [2, D], F32)
        xn = A.tile("xn4", [1, D], BF16)
        hT = A.tile("hT4", [1, 8, 128], BF16)
        sg = A.tile("sg", [D], F32)
        m12 = A.tile("m12", [D], F32)
        mg = A.tile("mg", [D], BF16)
        mT = A.tile("mT", [8, 128], BF16)
        tmp = A.tile("tmp4", [D], F32)
        if stop_after >= 4:
            def front4(t):
                s2 = t % 2
                dma("sp", xs[:, s2], x_own[t * 128:(t + 1) * 128, :], f"x4_{s2}", (), [("xs4", s2)])
                act(xn[:, 0], xs[:, s2], AF.Copy, [("xs4", s2), ("rstd_own", t)], [("xn4",)],
                    scale=rstd_own[:, t:t + 1])
                pT = PS(6, 1, BF16)
                for kc in range(8):
                    tr(pT[:, kc * 128:(kc + 1) * 128], xn[:, 0, kc * 128:(kc + 1) * 128], ident_b,
                       [("xn4",), ("ident_b",)], psk(6))
                for kc in range(8):
                    o = hT[:, 0, kc, :]
                    i = pT[:, kc * 128:(kc + 1) * 128]
                    act(o, i, AF.Identity, psk(6) + [("G1c",), ("modT", "a")], [("hT4", kc)],
                        scale=G1c[:, kc:kc + 1], bias=S1c[:, kc:kc + 1])

            def mid4(t):
                s2 = t % 2
                for bi, (wt, nm, srcT, skey, dst, dkey) in enumerate((
                        (WbrA, "WbrA", yaT, ("yaT",), m12, ("m12",)),
                        (WbrB, "WbrB", ybT, ("ybT",), tmp, ("tmp4",)))):
                    for c in range(2):
                        c0 = bi * 1024 + c * 512
                        for kc in range(8):
                            mm(PS(c), hT[:, 0, kc, :], Wg[:, kc, c0:c0 + 512], kc == 0, kc == 7,
                               [("hT4", kc), ("Wg",)], psk(c))
                    act(sg, PS(0, 2), AF.Sigmoid, psk(0, 2), [("sg",)])
                    for c in range(2):
                        for j in range(4):
                            mm(PS(2 + 2 * bi + c), srcT[:, j, t * 128:(t + 1) * 128], wt[:, j, c * 512:(c + 1) * 512],
                               j == 0, j == 3, [skey, (nm, 0), (nm, 1)], psk(2 + 2 * bi + c))
                    tt("dve", dst, sg, PS(2 + 2 * bi, 2), ALU.mult, [("sg",)] + psk(2 + 2 * bi, 2), [dkey])

            def tail4(t):
                s2 = t % 2
                tt("pool", mg, m12, tmp, ALU.add, [("m12",), ("tmp4",)], [("mg",)])
                pT2 = PS(7, 1, BF16)
                for kc in range(8):
                    tr(pT2[:, kc * 128:(kc + 1) * 128], mg[:, kc * 128:(kc + 1) * 128], ident_b,
                       [("mg",), ("ident_b",)], psk(7))
                cp("dve", mT, pT2.rearrange("p (k q) -> p k q", k=8), psk(7), [("mT",)])
                for c in range(2):
                    for kc in range(8):
                        mm(PS(c), mT[:, kc, :], Wo[:, kc, c * 512:(c + 1) * 512], kc == 0, kc == 7,
                           [("mT",), ("Wo",)], psk(c))
                tt("dve", tmp, PS(0, 2), gate1_bc, ALU.mult, psk(0, 2) + [("gate_bc", 0)], [("tmp4",)])
                tt("pool", x1[:, t, :], tmp, xs[:, s2], ALU.add, [("tmp4",), ("xs4", s2)], [("x1", t)])

            front4(0)
            for t in range(NOWN):
                mid4(t)
                if t + 1 < NOWN:
                    front4(t + 1)
                tail4(t)
        A.free("Wg", "WbrA", "WbrB", "Wo", "xs4", "xn4", "hT4", "sg", "m12", "mg", "mT", "tmp4", "yaT", "ybT")

        h2T = A.tile("h2T", [8, NOWN * 128], BF16)
        W1q = A.tile("W1q", [2, 8, 1024], BF16)
        W2q = A.tile("W2q", [2, 8, 1024], BF16)
        hidT = A.tile("hidT", [2, 8, 512], BF16)
        rbuf = A.tile("rbuf", [2, 512], F32)
        xn = A.tile("xn5", [1, D], BF16)
        sqj = A.tile("sqj5", [D], BF16)
        tmp = A.tile("tmp5", [1, D], F32)
        obuf = A.tile("obuf", [1, D], F32)
        if stop_after >= 5:
            w1v = w1.rearrange("(kc p) n -> p kc n", p=128)
            w2v = w2.rearrange("(hc p) n -> p hc n", p=128)

            def load_q(qh):
                qs_ = qh % 2
                dma("pool", W1q[:, qs_], w1v[:, :, qh * 1024:(qh + 1) * 1024], f"w1_{qs_}", (), [("W1q", qs_)])
                dma("pool", W2q[:, qs_], w2v[:, qh * 8:(qh + 1) * 8, :], f"w2_{qs_}", (), [("W2q", qs_)])

            load_q(0)
            load_q(1)
            for t in range(NOWN):
                s2 = t % 2
                ssc = stat[:, 64 + s2:65 + s2]
                act(sqj, x1[:, t, :], AF.Square, [("x1", t)], [("sqj5",), ("stat", "ss5", s2)], accum_out=ssc)
                rsc = stat[:, 66 + s2:67 + s2]
                act(rsc, ssc, AF.Sqrt, [("stat", "ss5", s2)], [("stat", "rs5", s2)], scale=1.0 / D, bias=EPS)
                recip(rsc, rsc, [("stat", "rs5", s2)], [("stat", "rs5", s2)])
                act(xn[:, 0], x1[:, t, :], AF.Copy, [("x1", t), ("stat", "rs5", s2)], [("xn5",)], scale=rsc)
                for kc in range(8):
                    bk = 6 + kc // 4
                    tr(PS(bk, 1, BF16)[:, (kc % 4) * 128:(kc % 4 + 1) * 128], xn[:, 0, kc * 128:(kc + 1) * 128],
                       ident_b, [("xn5",), ("ident_b",)], psk(bk))
                for kc in range(8):
                    bk = 6 + kc // 4
                    o = h2T[:, kc, t * 128:(t + 1) * 128]
                    i = PS(bk, 1, BF16)[:, (kc % 4) * 128:(kc % 4 + 1) * 128]
                    if bk == 6:
                        act(o, i, AF.Identity, psk(bk) + [("G2c",), ("modT", "b")], [("h2T", t // 4, kc)],
                            scale=G2c[:, kc:kc + 1], bias=S2c[:, kc:kc + 1])
                    else:
                        ts("dve", o, i, G2c[:, kc:kc + 1], S2c[:, kc:kc + 1], ALU.mult, ALU.add,
                           psk(bk) + [("G2c",), ("modT", "b")], [("h2T", t // 4, kc)])

            hcount = [0]

            def hid(qh, grp):
                qs_ = qh % 2
                hs = (qh * 4 + grp) % 2
                for hc in range(8):
                    hb = hcount[0] % 4
                    rb = hcount[0] % 2
                    hcount[0] += 1
                    for kc in range(8):
                        mm(PS(hb), W1q[:, qs_, kc, hc * 128:(hc + 1) * 128], h2T[:, kc, grp * 512:(grp + 1) * 512],
                           kc == 0, kc == 7, [("W1q", qs_), ("h2T", grp, kc)], psk(hb))
                    act(rbuf[:, rb, :], PS(hb), AF.Relu, psk(hb), [("rbuf", rb)])
                    tt("pool", hidT[:, hs, hc, :], rbuf[:, rb, :], rbuf[:, rb, :], ALU.mult,
                       [("rbuf", rb)], [("hidT", hs, hc)])

            ycount = [0]

            def ymm(qh, grp):
                qs_ = qh % 2
                hs = (qh * 4 + grp) % 2
                for tt_ in range(4):
                    t = grp * 4 + tt_
                    yb = 4 + 2 * (ycount[0] % 2)
                    ys = ycount[0] % 2
                    ycount[0] += 1
                    for c in range(2):
                        for hc in range(8):
                            mm(PS(yb + c), hidT[:, hs, hc, tt_ * 128:(tt_ + 1) * 128],
                               W2q[:, qs_, hc, c * 512:(c + 1) * 512], hc == 0, hc == 7,
                               [("hidT", hs, hc), ("W2q", qs_)], psk(yb + c))
                    tt("dve", tmp[:, 0, :], PS(yb, 2), gate2_bc, ALU.mult, psk(yb, 2) + [("gate_bc", 1)],
                       [("tmp5",)])
                    tt("pool", x1[:, t, :], x1[:, t, :], tmp[:, 0, :], ALU.add, [("x1", t), ("tmp5",)],
                       [("x1", t)])
                    if qh == 3:
                        ssc = stat[:, 68 + ys:69 + ys]
                        act(sqj, x1[:, t, :], AF.Square, [("x1", t)], [("sqj5",), ("stat", "ss6", ys)],
                            accum_out=ssc)
                        rsc = stat[:, 70 + ys:71 + ys]
                        act(rsc, ssc, AF.Sqrt, [("stat", "ss6", ys)], [("stat", "rs6", ys)], scale=1.0 / D, bias=EPS)
                        recip(rsc, rsc, [("stat", "rs6", ys)], [("stat", "rs6", ys)])
                        stt("dve", obuf[:, 0, :], x1[:, t, :], rsc, gf_bc, ALU.mult, ALU.mult,
                            [("x1", t), ("stat", "rs6", ys), ("gf_bc",)], [("obuf",)])
                        dma("sp", out[t * 128:(t + 1) * 128, :], obuf[:, 0, :], "o_0", [("obuf",)],
                            [("out", t)])

            seq = [(qh, grp) for qh in range(4) for grp in range(4)]
            hid(*seq[0])
            for n_, (qh, grp) in enumerate(seq):
                if n_ + 1 < len(seq):
                    hid(*seq[n_ + 1])
                ymm(qh, grp)
                if grp == 3 and qh + 2 < 4:
                    load_q(qh + 2)
            S.add("sp", lambda e: e.nop(), [("out", t) for t in range(NOWN)], ())

        for (nm, keys) in dumps:
            fv, dt_ = A.flat[nm]
            d_ap = nc.dram_tensor("dbg_" + nm, [128, fv.shape[1]], F32, kind="ExternalOutput").ap()
            dma("pool", d_ap, fv, "dbg_" + nm, list(keys), [("dbgout", nm)])
            S.add("sp", lambda e: e.nop(), [("dbgout", nm)], ())
        S.emit()
        nc._arena_peak = A.peak
    return nc


def _rope_tables():
    t = np.arange(SEQ)
    row = (t // 64).astype(np.float32)
    col = (t % 64).astype(np.float32)
    inv16 = (10000.0 ** (-np.arange(0, 32, 2, dtype=np.float32) / 32)).astype(np.float32)
    inv32 = (10000.0 ** (-np.arange(0, 64, 2, dtype=np.float32) / 64)).astype(np.float32)
    A_ = np.zeros((SEQ, 128), np.float32)
    for b, pos in enumerate((row, col)):
        ang = pos[:, None] * inv16[None, :]
        cs, sn = np.cos(ang).astype(np.float32), np.sin(ang).astype(np.float32)
        A_[:, b * 32:b * 32 + 16] = cs
        A_[:, b * 32 + 16:b * 32 + 32] = cs
        A_[:, 64 + b * 32:64 + b * 32 + 16] = -sn
        A_[:, 64 + b * 32 + 16:64 + b * 32 + 32] = sn
    B_ = np.zeros((SEQ, 128), np.float32)
    ang = t.astype(np.float32)[:, None] * inv32[None, :]
    cs, sn = np.cos(ang).astype(np.float32), np.sin(ang).astype(np.float32)
    B_[:, 0:32] = cs
    B_[:, 32:64] = cs
    B_[:, 64:96] = -sn
    B_[:, 96:128] = sn
    return A_, B_


def _masks(core):
    kl = np.arange(128)[:, None]
    ql = np.arange(128)[None, :]
    prev = np.where(kl >= ql, 0.0, NEG).astype(np.float32)
    nxt = np.where(kl <= ql, 0.0, NEG).astype(np.float32)
    allneg = np.full((128, 128), NEG, np.float32)
    first = allneg if core % 4 == 0 else prev
    last = allneg if core % 4 == 3 else nxt
    return np.stack([np.tile(m, (1, 4)) for m in (first, prev, nxt, last)]).astype(np.float32)


def make_in_maps(x, c, w_ada, b_ada, norm1_g, w_in, q_norm_a, k_norm_a, sink_b,
                 w_branch, w_out, norm2_g, w_mlp_in, w_mlp_out, final_g):
    f = lambda a: np.ascontiguousarray(np.asarray(a, dtype=np.float32))
    x = f(x); c = f(c)
    ropeA, ropeB = _rope_tables()
    pk = lambda v: f(np.asarray(v, np.float32).reshape(-1, 128).T)
    shared = {
        "w_ada": f(w_ada[0]), "bada_pk": pk(b_ada[0]), "g1_pk": pk(norm1_g[0]), "g2_pk": pk(norm2_g[0]),
        "gf": f(final_g), "w_in": f(w_in[0]), "qn_g": f(q_norm_a[0]), "kn_g": f(k_norm_a[0]),
        "sink": f(np.asarray(sink_b[0]).reshape(1, 8)), "w_br": f(w_branch[0]), "w_out": f(w_out[0]),
        "w1": f(w_mlp_in[0]), "w2": f(w_mlp_out[0]),
    }
    in_maps = []
    for core in range(N_CORES):
        b, qi = divmod(core, 4)
        q0 = qi * 2048
        own = slice(q0, q0 + 2048)
        oth_idx = np.concatenate([np.arange(0, q0), np.arange(q0 + 2048, SEQ)])
        halo = np.zeros((256, D), np.float32)
        ropeB_h = np.zeros((256, 128), np.float32)
        if q0 > 0:
            halo[0:128] = x[b, q0 - 128:q0]
            ropeB_h[0:128] = ropeB[q0 - 128:q0]
        if q0 + 2048 < SEQ:
            halo[128:256] = x[b, q0 + 2048:q0 + 2176]
            ropeB_h[128:256] = ropeB[q0 + 2048:q0 + 2176]
        m = dict(shared)
        m.update({
            "x_own": f(x[b, own]), "x_oth": f(x[b, oth_idx]), "x_halo": halo,
            "c_pk": pk(c[b]),
            "ropeA_own": f(ropeA[own]), "ropeA_oth": f(ropeA[oth_idx]),
            "ropeB_own": f(ropeB[own]), "ropeB_halo": ropeB_h,
            "masks": _masks(core),
        })
        in_maps.append(m)
    return in_maps


_NC_CACHE = {}


def kernel(x, c, w_ada, b_ada, norm1_g, w_in, q_norm_a, k_norm_a, sink_b,
           w_branch, w_out, norm2_g, w_mlp_in, w_mlp_out, final_g):
    in_maps = make_in_maps(x, c, w_ada, b_ada, norm1_g, w_in, q_norm_a, k_norm_a, sink_b,
                           w_branch, w_out, norm2_g, w_mlp_in, w_mlp_out, final_g)
    if "nc" not in _NC_CACHE:
        _NC_CACHE["nc"] = build()
    nc = _NC_CACHE["nc"]
    res = run_bass_kernel_spmd(nc, in_maps, core_ids=list(range(N_CORES)))
    outp = np.zeros((2, SEQ, D), np.float32)
    for core in range(N_CORES):
        b, qi = divmod(core, 4)
        outp[b, qi * 2048:(qi + 1) * 2048] = np.asarray(res.results[core]["out"], dtype=np.float32)
    return outp
```

```python
from contextlib import ExitStack
import os
import numpy as np
import concourse.bass as bass
import concourse.mybir as mybir
from concourse.bass_utils import run_bass_kernel_spmd

F32 = mybir.dt.float32
BF16 = mybir.dt.bfloat16
U8 = mybir.dt.uint8
ALU = mybir.AluOpType
AF = mybir.ActivationFunctionType
AX = mybir.AxisListType

N_CORES = 8
D = 1024
SEQ = 8192
NOWN = 16
NOTH = 48
EPS = 1e-6
NEG = -30000.0


class _Op:
    __slots__ = ("eng", "fn", "dma", "deps", "ticket", "has_dep", "idx", "dma_total")


class Sched:
    ENGS = ("pe", "act", "dve", "pool", "sp")

    def __init__(self, nc):
        self.nc = nc
        self.ops = []
        self.last_w = {}
        self.readers = {}
        self.dma_count = {}
        self.region_last = {}
        self.alias = {}

    def add(self, eng, fn, reads=(), writes=(), dma=None):
        op = _Op()
        op.eng, op.fn, op.dma = eng, fn, dma
        op.idx = len(self.ops)
        op.has_dep = False
        op.ticket = None
        op.dma_total = None
        cand = {}
        ps_r = [k for k in reads if k[0] == "ps"]
        if ps_r:
            reads = [k for k in reads if k[0] != "ps"]
            writes = list(writes) + [k for k in ps_r if k not in writes]
        for k in reads:
            w = self.last_w.get(k)
            if w is not None:
                cand[w] = "raw"
        for k in writes:
            w = self.last_w.get(k)
            if w is not None:
                cand.setdefault(w, "waw")
            for r in self.readers.get(k, {}).values():
                cand.setdefault(r, "war")
        regions = set(k[0] for k in reads) | set(k[0] for k in writes)
        for r in regions:
            for old in self.alias.get(r, ()):
                for idx in self.region_last.get(old, {}).values():
                    cand.setdefault(idx, "alias")
        best = {}
        for p, kind in cand.items():
            pop = self.ops[p]
            if pop.dma is None and dma is None and pop.eng == eng:
                if eng == "pe":
                    continue
            key = ("d", pop.dma) if pop.dma is not None else ("e", pop.eng)
            if p > best.get(key, -1):
                best[key] = p
        op.deps = sorted(best.values())
        for p in op.deps:
            self.ops[p].has_dep = True
        mykey = ("d", dma) if dma is not None else ("e", eng)
        for k in writes:
            self.last_w[k] = op.idx
            self.readers[k] = {}
        for k in reads:
            if k not in writes:
                self.readers.setdefault(k, {})[mykey] = op.idx
        for r in regions:
            self.region_last.setdefault(r, {})[mykey] = op.idx
        if dma is not None:
            self.dma_count[dma] = self.dma_count.get(dma, 0) + 1
            op.dma_total = 16 * self.dma_count[dma]
        self.ops.append(op)
        return op

    def emit(self):
        nc = self.nc
        cnt = {e: 0 for e in self.ENGS}
        for op in self.ops:
            if op.dma is None and op.has_dep:
                cnt[op.eng] += 1
                op.ticket = cnt[op.eng]
        with ExitStack() as es:
            esem = {e: es.enter_context(nc.semaphore("s_" + e)) for e in self.ENGS}
            dsem = {d: es.enter_context(nc.semaphore("d_" + str(d))) for d in self.dma_count}
            block = es.enter_context(nc.Block())
            ops = self.ops

            def make(engname):
                def body(eng):
                    waited = {}
                    for op in ops:
                        if op.eng != engname:
                            continue
                        need = {}
                        for p in op.deps:
                            pop = ops[p]
                            if pop.dma is not None:
                                key, val = ("d", pop.dma), pop.dma_total
                            else:
                                key, val = ("e", pop.eng), pop.ticket
                            if val > need.get(key, 0):
                                need[key] = val
                        for key, val in need.items():
                            if waited.get(key, 0) >= val:
                                continue
                            waited[key] = val
                            sem = dsem[key[1]] if key[0] == "d" else esem[key[1]]
                            eng.wait_ge(sem, val)
                        ins = op.fn(eng)
                        if op.dma is not None:
                            ins.then_inc(dsem[op.dma], 16)
                        elif op.has_dep:
                            ins.then_inc(esem[engname], 1)
                return body

            block.tensor(make("pe"))
            block.scalar(make("act"))
            block.vector(make("dve"))
            block.gpsimd(make("pool"))
            block.sync(make("sp"))


class Arena:
    def __init__(self, S, tensor, total):
        self.S, self.t, self.total = S, tensor, total
        self.live = {}
        self.dead = []
        self.peak = 0
        self.flat = {}

    def alloc(self, name, nbytes, at=None):
        nbytes = (nbytes + 63) // 64 * 64
        pos = 0
        if at is not None:
            pos = at
            for n_, (o, s) in self.live.items():
                assert not (o < pos + nbytes and pos < o + s), f"{name}@{at} overlaps live {n_} {(o, s)}"
        else:
            for (o, s) in sorted(self.live.values()):
                if o - pos >= nbytes:
                    break
                pos = max(pos, o + s)
        if pos + nbytes > self.total:
            raise RuntimeError(f"arena overflow allocating {name} {nbytes} at {pos}; live={self.live}")
        assert name not in self.live and name not in self.S.alias
        self.live[name] = (pos, nbytes)
        self.peak = max(self.peak, pos + nbytes)
        self.S.alias[name] = [n for (o, s, n) in self.dead if o < pos + nbytes and pos < o + s]
        return pos

    def free(self, *names):
        for name in names:
            o, s = self.live.pop(name)
            self.dead.append((o, s, name))

    def top(self):
        return max(o + s for (o, s) in self.live.values())

    def tile(self, name, free_shape, dt, at=None):
        n = int(np.prod(free_shape))
        sz = 4 if dt == F32 else 2
        off = self.alloc(name, n * sz, at)
        v = self.t[:, off:off + n * sz].bitcast(dt)
        self.flat[name] = (v, dt)
        if len(free_shape) == 2:
            v = v.rearrange("p (a b) -> p a b", a=free_shape[0])
        elif len(free_shape) == 3:
            v = v.rearrange("p (a b c) -> p a b c", a=free_shape[0], b=free_shape[1])
        return v


def build(stop_after=99, dumps=(), p1=(NOWN, 2, NOTH), p1parts=15):
    nc = bass.Bass("TRN2", target_bir_lowering=False)

    def din(name, shape):
        return nc.dram_tensor(name, list(shape), F32, kind="ExternalInput").ap()

    x_own = din("x_own", [NOWN * 128, D])
    x_oth = din("x_oth", [NOTH * 128, D])
    x_halo = din("x_halo", [256, D])
    c_pk = din("c_pk", [128, 8])
    w_ada = din("w_ada", [D, 6 * D])
    bada_pk = din("bada_pk", [128, 48])
    g1_pk = din("g1_pk", [128, 8])
    g2_pk = din("g2_pk", [128, 8])
    gf = din("gf", [D])
    w_in = din("w_in", [D, 3584])
    qn_g = din("qn_g", [64])
    kn_g = din("kn_g", [64])
    sink = din("sink", [1, 8])
    w_br = din("w_br", [2, 512, D])
    w_out = din("w_out", [D, D])
    w1 = din("w1", [D, 4 * D])
    w2 = din("w2", [4 * D, D])
    ropeA_own = din("ropeA_own", [NOWN * 128, 128])
    ropeA_oth = din("ropeA_oth", [NOTH * 128, 128])
    ropeB_own = din("ropeB_own", [NOWN * 128, 128])
    ropeB_halo = din("ropeB_halo", [256, 128])
    masks = din("masks", [4, 128, 512])
    out = nc.dram_tensor("out", [NOWN * 128, D], F32, kind="ExternalOutput").ap()

    ARENA_BYTES = 212736
    with ExitStack() as es:
        arena_t = es.enter_context(nc.sbuf_tensor("arena", [128, ARENA_BYTES], U8))
        psum_t = es.enter_context(nc.psum_tensor("psum", [128, 16384], U8))
        S = Sched(nc)
        A = Arena(S, arena_t, ARENA_BYTES)

        def PS(b0, nb=1, dt=F32):
            return psum_t[:, b0 * 2048:(b0 + nb) * 2048].bitcast(dt)

        def psk(b0, nb=1):
            return [("ps", b) for b in range(b0, b0 + nb)]

        def mm(o, lhsT, rhs, start, stop, reads, writes):
            S.add("pe", lambda e: e.matmul(o, lhsT=lhsT, rhs=rhs, start=start, stop=stop), reads, writes)

        def tr(o, i, ident, reads, writes):
            S.add("pe", lambda e: e.transpose(out=o, in_=i, identity=ident), reads, writes)

        def act(o, i, func, reads, writes, **kw):
            S.add("act", lambda e: e.activation(out=o, in_=i, func=func, **kw), reads, writes)

        def tt(eng, o, i0, i1, op, reads, writes):
            S.add(eng, lambda e: e.tensor_tensor(out=o, in0=i0, in1=i1, op=op), reads, writes)

        def ts(eng, o, i0, s1, s2, op0, op1, reads, writes):
            S.add(eng, lambda e: e.tensor_scalar(out=o, in0=i0, scalar1=s1, scalar2=s2, op0=op0, op1=op1), reads, writes)

        def stt(eng, o, i0, sc, i1, op0, op1, reads, writes):
            S.add(eng, lambda e: e.scalar_tensor_tensor(out=o, in0=i0, scalar=sc, in1=i1, op0=op0, op1=op1), reads, writes)

        def cp(eng, o, i, reads, writes):
            S.add(eng, lambda e: e.tensor_copy(out=o, in_=i), reads, writes)

        def recip(o, i, reads, writes):
            S.add("dve", lambda e: e.reciprocal(out=o, in_=i), reads, writes)

        def memset(eng, o, val, writes):
            S.add(eng, lambda e: e.memset(o, val), (), writes)

        def dma(eng, o, i, sem, reads, writes):
            S.add(eng, lambda e: e.dma_start(out=o, in_=i), reads, writes, dma=sem)

        ident_f = A.tile("ident_f", [128], F32)
        ident_b = A.tile("ident_b", [128], BF16)
        ones_f = A.tile("ones_f", [64], F32)
        modT = A.tile("modT", [48], F32)
        G1c = A.tile("G1c", [8], F32)
        G2c = A.tile("G2c", [8], F32)
        gate1_bc = A.tile("gate1_bc", [D], F32)
        gate2_bc = A.tile("gate2_bc", [D], F32)
        gf_bc = A.tile("gf_bc", [D], F32)
        rstd_own = A.tile("rstd_own", [NOWN], F32)
        stat = A.tile("stat", [128], F32)
        Y_OFF = A.top()
        wa = A.tile("wa", [2, 8, 512], F32)
        X1_OFF = A.top()
        qn_bc = A.tile("qn_bc", [64], F32)
        kn_bc = A.tile("kn_bc", [64], F32)
        sel = A.tile("sel", [192], BF16)
        knsw_bc = A.tile("knsw_bc", [64], F32)
        qnsw_bc = A.tile("qnsw_bc", [64], F32)
        sc = A.tile("sc", [8], F32)
        csb = A.tile("csb", [8], F32)
        bada = A.tile("bada", [48], F32)
        g1c = A.tile("g1c", [8], F32)
        g2c = A.tile("g2c", [8], F32)
        sinkt = A.tile("sinkt", [8], F32)
        esf = A.tile("esf", [8], F32)
        gcolb = A.tile("gcolb", [2, 128], F32)

        memset("pool", ident_f, 0.0, [("ident_f",)])
        S.add("pool", lambda e: e.affine_select(out=ident_f, in_=ident_f, pattern=[[-1, 128]],
                                                compare_op=ALU.not_equal, fill=1.0, base=0,
                                                channel_multiplier=1),
              [("ident_f",)], [("ident_f",)])
        cp("dve", ident_b, ident_f, [("ident_f",)], [("ident_b",)])
        memset("pool", ones_f, 1.0, [("ones_f",)])
        memset("pool", sel, 0.0, [("sel",)])
        memset("pool", sel[:, 64:128], 1.0, [("sel",)])

        dma("sp", csb, c_pk, "c_c", (), [("csb",)])
        dma("sp", bada, bada_pk, "c_bada", (), [("bada",)])
        dma("sp", g1c, g1_pk, "c_g1", (), [("g1c",)])
        act(sc, csb, AF.Silu, [("csb",)], [("sc",)])

        w_ada_v = w_ada.rearrange("(kc p) n -> p kc n", p=128)
        MODBANK = 7
        modps = PS(MODBANK)[:, 320:368]

        def mod_dma(blk):
            sl = blk % 2
            dma("sp", wa[:, sl], w_ada_v[:, :, blk * 512:(blk + 1) * 512], f"wa{sl}", (), [("wa", sl)])

        def mod_block(blk):
            sl = blk % 2
            for jj in range(4):
                j = blk * 4 + jj
                for kc in range(8):
                    mm(modps[:, j:j + 1], wa[:, sl, kc, jj * 128:(jj + 1) * 128], sc[:, kc:kc + 1],
                       kc == 0, kc == 7, [("wa", sl), ("sc",)], psk(MODBANK))

        mod_dma(0)
        mod_dma(1)
        for blk in range(4):
            mod_block(blk)
            mod_dma(blk + 2)
        tt("dve", modT[:, 0:16], modps[:, 0:16], bada[:, 0:16], ALU.add, psk(MODBANK) + [("bada",)], [("modT", "a")])
        S1c = modT[:, 0:8]
        S2c = modT[:, 24:32]
        stt("dve", G1c, modT[:, 8:16], 1.0, g1c, ALU.add, ALU.mult, [("modT", "a"), ("g1c",)], [("G1c",)])

        dma("sp", g2c, g2_pk, "c_g2", (), [("g2c",)])
        dma("sp", sinkt[0:1, :], sink, "c_sink", (), [("sinkt",)])
        dma("sp", gf_bc, gf.partition_broadcast(128), "c_gf", (), [("gf_bc",)])
        dma("sp", qn_bc, qn_g.partition_broadcast(128), "c_qn", (), [("qn_bc",)])
        dma("sp", kn_bc, kn_g.partition_broadcast(128), "c_kn", (), [("kn_bc",)])
        for (src_, dst_, sk, dk) in ((kn_bc, knsw_bc, "kn_bc", "knsw_bc"), (qn_bc, qnsw_bc, "qn_bc", "qnsw_bc")):
            knv = src_.rearrange("p (b t i) -> p b t i", b=2, t=2)
            ksv = dst_.rearrange("p (b t i) -> p b t i", b=2, t=2)
            for t_ in range(2):
                cp("pool", ksv[:, :, t_, :], knv[:, :, 1 - t_, :], [(sk,)], [(dk, t_)])

        KT_B = A.tile("KT_B", [18 * 128], BF16)
        V_B = A.tile("V_B", [18, 192], BF16)
        QT_B = A.tile("QT_B", [4, NOWN * 128], BF16)
        KT_A = A.tile("KT_A", [SEQ], BF16)
        V_A = A.tile("V_A", [64, 192], BF16)
        QT_A = A.tile("QT_A", [4, NOWN * 128], BF16)
        A_END = A.top()
        assert A_END - 16 * 1024 >= X1_OFF + 64 * 1024
        memset("pool", V_A[:, :, 64:128], 1.0, [("V_A",)])
        memset("pool", V_B[:, :, 64:128], 1.0, [("V_B",)])

        Wb = A.tile("Wb", [8, 1536], BF16)
        dma("pool", Wb, w_in.rearrange("(kc p) n -> p kc n", p=128)[:, :, 0:1536], "w_in", (), [("Wb",)])
        xs = A.tile("xs", [2, D], F32)
        rp = A.tile("rp", [3, 256], F32)
        rpg = A.tile("rpg", [3, 4, 64], F32)
        xn = A.tile("xn", [1, D], BF16)
        hT = A.tile("hT", [2, 8, 128], BF16)
        wkq = A.tile("wkq", [2, 3, 512], F32)
        wkk = A.tile("wkk", [3, 3, 128], F32)
        qrq = A.tile("qrq", [6, 512], BF16)
        qrk = A.tile("qrk", [6, 128], BF16)

        tiles = ([("own", t) for t in range(p1[0])] + [("halo", h) for h in range(p1[1])]
                 + [("oth", u) for u in range(p1[2])])
        NT = len(tiles)

        def stageA(ti):
            kind, idx = tiles[ti]
            s3, s2 = ti % 3, ti % 2
            sx = ti % 2
            if kind == "own":
                xa = x_own[idx * 128:(idx + 1) * 128, :]
                ropes = [ropeA_own[idx * 128:(idx + 1) * 128, :], ropeB_own[idx * 128:(idx + 1) * 128, :]]
            elif kind == "oth":
                xa = x_oth[idx * 128:(idx + 1) * 128, :]
                ropes = [ropeA_oth[idx * 128:(idx + 1) * 128, :]]
            else:
                xa = x_halo[idx * 128:(idx + 1) * 128, :]
                ropes = [ropeB_halo[idx * 128:(idx + 1) * 128, :]]
            dma("sp", xs[:, sx], xa, f"xs{sx}", (), [("xs", sx)])
            off = 0
            for rap in ropes:
                dma("sp", rp[:, s3, off:off + 128], rap, f"rp{s3}_{off}", (), [("rp", s3, off)])
                off += 128
            ssc = stat[:, s2:s2 + 1]
            act(xn[:, 0], xs[:, sx], AF.Square, [("xs", sx)], [("xn",), ("stat", "ss", s2)], accum_out=ssc)
            rsc = stat[:, 2 + s2:3 + s2]
            act(rsc, ssc, AF.Ln, [("stat", "ss", s2)], [("stat", "rs", s2)], scale=1.0 / D, bias=EPS)
            if kind == "own":
                rstd, rkey = rstd_own[:, idx:idx + 1], ("rstd_own", idx)
            else:
                rstd, rkey = stat[:, 4 + s2:5 + s2], ("stat", "rstd", s2)
            act(rstd, rsc, AF.Exp, [("stat", "rs", s2)], [rkey], scale=-0.5)
            act(xn[:, 0], xs[:, sx], AF.Copy, [("xs", sx), rkey], [("xn",)], scale=rstd)
            if kind != "halo":
                tt("pool", rpg[:, s3, 0, :], rp[:, s3, 0:64], kn_bc, ALU.mult, [("rp", s3, 0), ("kn_bc",)],
                   [("rpg", s3, 0)])
                tt("pool", rpg[:, s3, 1, :], rp[:, s3, 64:128], knsw_bc, ALU.mult,
                   [("rp", s3, 0), ("knsw_bc", 0), ("knsw_bc", 1)], [("rpg", s3, 1)])
            if kind == "own":
                tt("pool", rpg[:, s3, 2, :], rp[:, s3, 0:64], qn_bc, ALU.mult, [("rp", s3, 0), ("qn_bc",)],
                   [("rpg", s3, 2)])
                tt("pool", rpg[:, s3, 3, :], rp[:, s3, 64:128], qnsw_bc, ALU.mult,
                   [("rp", s3, 0), ("qnsw_bc", 0), ("qnsw_bc", 1)], [("rpg", s3, 3)])

        def hbank(kc):
            return (0, kc * 128) if kc < 2 else (1, (kc - 2) * 128)

        def stageB(ti):
            s2 = ti % 2
            for kc in range(8):
                bk, c0 = hbank(kc)
                tr(PS(bk, 1, BF16)[:, c0:c0 + 128], xn[:, 0, kc * 128:(kc + 1) * 128], ident_b,
                   [("xn",), ("ident_b",)], psk(bk))
            for kc in range(8):
                bk, c0 = hbank(kc)
                o = hT[:, s2, kc, :]
                i = PS(bk, 1, BF16)[:, c0:c0 + 128]
                if bk == 0:
                    act(o, i, AF.Identity, psk(bk) + [("G1c",), ("modT", "a")], [("hT", s2, kc)],
                        scale=G1c[:, kc:kc + 1], bias=S1c[:, kc:kc + 1])
                else:
                    ts("dve", o, i, G1c[:, kc:kc + 1], S1c[:, kc:kc + 1], ALU.mult, ALU.add,
                       psk(bk) + [("G1c",), ("modT", "a")], [("hT", s2, kc)])

        def proj(s2, c0, c1, bank):
            for kc in range(8):
                mm(PS(bank)[:, 0:c1 - c0], hT[:, s2, kc, :], Wb[:, kc, c0:c1], kc == 0, kc == 7,
                   [("hT", s2, kc), ("Wb",)], psk(bank))

        def chain(ps_view, bank, H, norm, gains, tabC, tabS, tab_keys, nb, wf, wkey, scol, qr_v, qr_key,
                  trbank, trcol, final):
            n = H * 64
            hw = 32 // nb
            v3 = lambda a: a.rearrange("p (h d) -> p h d", h=H)
            f0, f1, f2 = wf[0][:, 0:n], wf[1][:, 0:n], wf[2][:, 0:n]
            k0, k1, k2 = wkey + (0,), wkey + (1,), wkey + (2,)
            act(f0, ps_view, AF.Copy, psk(bank), [k0])
            yield
            tt("pool", v3(f2), v3(f0), tabC.unsqueeze(1).to_broadcast([128, H, 64]), ALU.mult,
               [k0] + tab_keys, [k2])
            if norm:
                ssh = stat[:, scol:scol + H]
                rh = stat[:, scol + 8:scol + 8 + H]
                for h_ in range(H):
                    act(f1[:, h_ * 64:(h_ + 1) * 64], ps_view[:, h_ * 64:(h_ + 1) * 64], AF.Square, psk(bank),
                        [k1, ("stat", "ssh", scol)], accum_out=ssh[:, h_:h_ + 1])
            yield
            if norm:
                act(rh, ssh, AF.Ln, [("stat", "ssh", scol)], [("stat", "rh", scol)], scale=1.0 / 64, bias=EPS)
                act(rh, rh, AF.Exp, [("stat", "rh", scol)], [("stat", "rh", scol)], scale=-0.5)
            sv = f0.rearrange("p (h b t i) -> p h b t i", h=H, b=nb, t=2)
            bv = f1.rearrange("p (h b t i) -> p h b t i", h=H, b=nb, t=2)
            Sv = tabS.rearrange("p (b t i) -> p b t i", b=nb, t=2)
            for t_ in range(2):
                tt("pool", bv[:, :, :, t_, :], sv[:, :, :, 1 - t_, :],
                   Sv[:, :, t_, :].unsqueeze(1).to_broadcast([128, H, nb, hw]), ALU.mult,
                   [k0] + tab_keys, [k1])
            yield
            if H == 8:
                p4 = lambda a: a.rearrange("p (g j d) -> p g j d", g=2, j=4)
                va, vb = p4(f2), p4(f1)
                ov = qr_v.rearrange("p (j g d) -> p g j d", j=4, g=2)
            else:
                va, vb, ov = v3(f2), v3(f1), v3(qr_v)
            if norm:
                tt("dve", va, va, vb, ALU.add, [k1, k2], [k2])
                yield
                if H == 8:
                    rb = rh.rearrange("p (g j) -> p g j", g=2).unsqueeze(3).to_broadcast([128, 2, 4, 64])
                else:
                    rb = rh.unsqueeze(2).to_broadcast([128, H, 64])
                tt("dve", ov, va, rb, ALU.mult, [k2, ("stat", "rh", scol)], [qr_key])
            else:
                tt("dve", ov, va, vb, ALU.add, [k1, k2], [qr_key])
            yield

        def chain_tail(H, qr_v, qr_key, trbank, trcol, final):
            n = H * 64
            pTv = PS(trbank, 1, BF16)
            for j in range(n // 128):
                tr(pTv[:, trcol + j * 128:trcol + (j + 1) * 128], qr_v[:, j * 128:(j + 1) * 128], ident_b,
                   [qr_key, ("ident_b",)], psk(trbank))
            dst, dkey = final
            if n == 512:
                cp("dve", dst, pTv[:, trcol:trcol + 512].rearrange("p (j q) -> p j q", j=4), psk(trbank), [dkey])
            else:
                cp("dve", dst, pTv[:, trcol:trcol + 128], psk(trbank), [dkey])

        def run_chains(gens):
            gens = list(gens)
            while gens:
                for g_ in list(gens):
                    try:
                        next(g_)
                    except StopIteration:
                        gens.remove(g_)

        def vcopy(dst3, bank, vkey):
            act(dst3.rearrange("p (a d) -> p a d", a=3)[:, 0:3:2, :],
                PS(bank)[:, 128:256].rearrange("p (a d) -> p a d", a=2), AF.Copy, psk(bank), [vkey])

        def stageC(ti):
            kind, idx = tiles[ti]
            s3, s2 = ti % 3, ti % 2
            rA = rp[:, s3, 0:128]
            rB = rp[:, s3, 128:256] if kind == "own" else rp[:, s3, 0:128]
            kA = [("rp", s3, 0)]
            kB = [("rp", s3, 128)] if kind == "own" else [("rp", s3, 0)]
            kG = [("rpg", s3, 0), ("rpg", s3, 1)]
            gens = []
            tails = []
            par = ti % 3

            def add(ps_view, bank, H, norm, tabC, tabS, tkeys, nb, wslot, wk_t, wname, scol, qr_t, qname, qslot,
                    trbank, trcol, final):
                wf = [wk_t[:, wslot, f, :] for f in range(3)]
                qv = qr_t[:, qslot, :]
                gens.append(chain(ps_view, bank, H, norm, None, tabC, tabS, tkeys, nb, wf, (wname, wslot), scol,
                                  qv, (qname, qslot), trbank, trcol, final))
                tails.append(lambda: chain_tail(H, qv, (qname, qslot), trbank, trcol, final))

            if kind == "own":
                t = idx
                proj(s2, 0, 512, 2)
                proj(s2, 512, 768, 3)
                proj(s2, 768, 1280, 4)
                proj(s2, 1280, 1536, 5)
                vcopy(V_A[:, t, :], 3, ("V_A",))
                vcopy(V_B[:, t + 1, :], 5, ("V_B",))
                kQ = [("rpg", s3, 2), ("rpg", s3, 3)]
                add(PS(2), 2, 8, True, rpg[:, s3, 2, :], rpg[:, s3, 3, :], kQ, 2, 0, wkq, "wkq", 8,
                    qrq, "qrq", 0 + par, 6, 0, (QT_A[:, :, t * 128:(t + 1) * 128], ("QT_A",)))
                add(PS(3)[:, 0:128], 3, 2, True, rpg[:, s3, 0, :], rpg[:, s3, 1, :], kG, 2, 0, wkk, "wkk", 24,
                    qrk, "qrk", 0 + par, 6, 512, (KT_A[:, t * 128:(t + 1) * 128], ("KT_A",)))
                add(PS(4), 4, 8, False, rB[:, 0:64], rB[:, 64:128], kB, 1, 1, wkq, "wkq", 0,
                    qrq, "qrq", 3 + par, 7, 0, (QT_B[:, :, t * 128:(t + 1) * 128], ("QT_B",)))
                add(PS(5)[:, 0:128], 5, 2, False, rB[:, 0:64], rB[:, 64:128], kB, 1, 1, wkk, "wkk", 0,
                    qrk, "qrk", 3 + par, 7, 512, (KT_B[:, (t + 1) * 128:(t + 2) * 128], ("KT_B",)))
            elif kind == "oth":
                kt = NOWN + idx
                bank = 2 + ti % 4
                w_ = (0, 2)[ti % 2]
                proj(s2, 512, 768, bank)
                vcopy(V_A[:, kt, :], bank, ("V_A",))
                add(PS(bank)[:, 0:128], bank, 2, True, rpg[:, s3, 0, :], rpg[:, s3, 1, :], kG, 2, w_, wkk, "wkk",
                    24 + 16 * (ti % 2), qrk, "qrk", 0 + par, 6 + ti % 2, 0,
                    (KT_A[:, kt * 128:(kt + 1) * 128], ("KT_A",)))
            else:
                kb = 0 if idx == 0 else 17
                proj(s2, 1280, 1536, 5)
                vcopy(V_B[:, kb, :], 5, ("V_B",))
                add(PS(5)[:, 0:128], 5, 2, False, rB[:, 0:64], rB[:, 64:128], kB, 1, 1, wkk, "wkk", 0,
                    qrk, "qrk", 3 + par, 7, 512, (KT_B[:, kb * 128:(kb + 1) * 128], ("KT_B",)))
            run_chains(gens)
            return tails

        MOD_EVERY = 1
        if stop_after >= 1:
            next_blk = 4
            tailq = [[], []]
            for step in range(NT + 4):
                for tl in tailq.pop(0):
                    tl()
                if 1 <= step <= NT:
                    stageB(step - 1)
                if step < NT:
                    stageA(step)
                new_tails = []
                if 2 <= step <= NT + 1:
                    new_tails = stageC(step - 2)
                tailq.append(new_tails)
                if step >= 2 and step % MOD_EVERY == 0 and next_blk < 12:
                    mod_block(next_blk)
                    if next_blk + 2 < 12:
                        mod_dma(next_blk + 2)
                    next_blk += 1
            while next_blk < 12:
                mod_block(next_blk)
                if next_blk + 2 < 12:
                    mod_dma(next_blk + 2)
                next_blk += 1
        else:
            for blk in range(4, 12):
                mod_block(blk)
                if blk + 2 < 12:
                    mod_dma(blk + 2)
        tt("dve", modT[:, 16:48], modps[:, 16:48], bada[:, 16:48], ALU.add, psk(MODBANK) + [("bada",)], [("modT", "b")])
        stt("dve", G2c, modT[:, 32:40], 1.0, g2c, ALU.add, ALU.mult, [("modT", "b"), ("g2c",)], [("G2c",)])
        act(esf[0:1, :], sinkt[0:1, :], AF.Exp, [("sinkt",)], [("esf",)])
        for gi, (gbc, base, bank) in enumerate(((gate1_bc, 16, 1), (gate2_bc, 40, 3))):
            gps = PS(bank, 2)
            for cc in range(8):
                sl = cc % 2
                cp("dve", gcolb[:, sl], modT[:, base + cc:base + cc + 1].to_broadcast([128, 128]),
                   [("modT", "b")], [("gcolb", sl)])
                mm(gps[:, cc * 128:(cc + 1) * 128], gcolb[:, sl], ident_f, True, True,
                   [("gcolb", sl), ("ident_f",)], psk(bank + cc // 4))
            cp("dve", gbc, gps, psk(bank, 2), [("gate_bc", gi)])
        A.free("Wb", "xs", "rp", "rpg", "xn", "hT", "wkq", "wkk", "qrq", "qrk", "wa",
               "sc", "csb", "bada", "g1c", "g2c", "sinkt", "gcolb", "knsw_bc", "qnsw_bc")
        yaT = A.tile("yaT", [4, NOWN * 128], BF16, at=Y_OFF)
        ybT = A.tile("ybT", [4, NOWN * 128], BF16, at=Y_OFF + 16 * 1024)
        maskb = A.tile("maskb", [4, 512], BF16, at=ARENA_BYTES - 4096)
        esink = A.tile("esink", [8, 128], BF16, at=ARENA_BYTES - 4096 - 2048)
        dma("pool", maskb, masks.rearrange("m p n -> p m n"), "c_mask", (), [("maskb",)])
        cp("dve", esink[0:1, :, :], esf[0:1, :].unsqueeze(2).to_broadcast([1, 8, 128]), [("esf",)], [("esink",)])
        A.free("esf")
        Wg = A.tile("Wg", [8, 2048], BF16)
        if stop_after >= 4:
            dma("pool", Wg, w_in.rearrange("(kc p) n -> p kc n", p=128)[:, :, 1536:3584], "w_g", (), [("Wg",)])

        SCALE = 0.125
        PT = A.tile("PT", [3, 1024], BF16)
        osb = A.tile("osb", [2, 2, 512], F32)
        rrow = A.tile("rrow", [2, 512], F32)
        P2_END = max(A.live[n_][0] + A.live[n_][1] for n_ in ("Wg", "PT", "osb", "rrow"))

        def norm_part1(qt, par, obanks, gsel):
            for g in gsel:
                r = 64 if g == 0 else 0
                cp("dve", osb[:, par, g, :], PS(obanks[g]), psk(obanks[g]), [("osb", par, g)])
                recip(rrow[r:r + 1, par, :], osb[r:r + 1, par, g, :],
                      [("osb", par, g)], [("rrow", par, g)])

        def norm_part2(qt, par, bcbanks, gsel, dstT, dkey):
            for g in gsel:
                r = 64 if g == 0 else 0
                bcbank = bcbanks[g]
                mm(PS(bcbank)[g * 64:(g + 1) * 64, :], ones_f[r:r + 1, 0:64],
                   rrow[r:r + 1, par, :], True, True,
                   [("rrow", par, g), ("ones_f",)], psk(bcbank))
                tt("dve", dstT[g * 64:(g + 1) * 64, :, qt * 128:(qt + 1) * 128],
                   osb[g * 64:(g + 1) * 64, par, g, :].rearrange("p (j q) -> p j q", j=4),
                   PS(bcbank)[g * 64:(g + 1) * 64, :].rearrange("p (j q) -> p j q", j=4), ALU.mult,
                   [("osb", par, g)] + psk(bcbank), [dkey])

        if stop_after >= 2:
            NKT = 64
            AHEAD = 2

            def qk2(it):
                qt, kt = divmod(it, NKT)
                sb_ = it % 3
                for g in range(2):
                    mm(PS(2 * sb_ + g), KT_A[g * 64:(g + 1) * 64, kt * 128:(kt + 1) * 128],
                       QT_A[g * 64:(g + 1) * 64, :, qt * 128:(qt + 1) * 128], True, True,
                       [("KT_A",), ("QT_A",)], psk(2 * sb_ + g))

            deferred = {}
            total = NOWN * NKT
            for it in range(min(AHEAD, total)):
                qk2(it)
            for it in range(total):
                qt, kt = divmod(it, NKT)
                sb_, pb = it % 3, it % 3
                act(PT[:, pb, :], PS(2 * sb_, 2), AF.Exp, psk(2 * sb_, 2), [("PT", pb)], scale=SCALE)
                if it + AHEAD < total:
                    qk2(it + AHEAD)
                for g in range(2):
                    mm(PS(6 + g), V_A[:, kt, g * 64:g * 64 + 128], PT[:, pb, g * 512:(g + 1) * 512],
                       kt == 0, kt == NKT - 1, [("PT", pb), ("V_A",)], psk(6 + g))
                if kt == NKT - 1:
                    par = qt % 2
                    norm_part1(qt, par, (6, 7), (0, 1))
                    deferred[it + 2] = (lambda qt=qt, par=par: norm_part2(qt, par, (0, 2), (0, 1), yaT, ("yaT",)))
                if it in deferred:
                    deferred.pop(it)()
            for k in sorted(deferred):
                deferred[k]()
        A.free("PT", "KT_A", "V_A", "QT_A")
        Wo = A.tile("Wo", [8, D], BF16, at=A_END - 16 * 1024)
        WbrA = A.tile("WbrA", [4, D], BF16, at=P2_END + 8 * 1024)
        WbrB = A.tile("WbrB", [4, D], BF16, at=P2_END)
        if stop_after >= 4:
            for bi, (wt, nm) in enumerate(((WbrA, "WbrA"), (WbrB, "WbrB"))):
                src = w_br[bi].rearrange("(g j d) n -> g d j n", g=2, j=4)
                for g in range(2):
                    dma("pool", wt[g * 64:(g + 1) * 64, :, :], src[g], "w_" + nm, (), [(nm, g)])
            dma("pool", Wo, w_out.rearrange("(kc p) n -> p kc n", p=128), "w_o", (), [("Wo",)])

        PTB = A.tile("PTB", [2, 1536], BF16)
        if stop_after >= 3:
            def qk3(i):
                qt, g = divmod(i, 2)
                sb_ = i % 2
                for blk in range(3):
                    kb = qt + blk
                    bank = 3 * sb_ + blk
                    mm(PS(bank), KT_B[g * 64:(g + 1) * 64, kb * 128:(kb + 1) * 128],
                       QT_B[g * 64:(g + 1) * 64, :, qt * 128:(qt + 1) * 128], True, blk == 1,
                       [("KT_B",), ("QT_B",)], psk(bank))
                for blk in (0, 2):
                    bank = 3 * sb_ + blk
                    mi = (0 if qt == 0 else 1) if blk == 0 else (3 if qt == NOWN - 1 else 2)
                    mm(PS(bank), ident_b, maskb[:, mi, :], False, True,
                       [("ident_b",), ("maskb",)], psk(bank))

            total = NOWN * 2
            deferred = {}
            qk3(0)
            for i in range(total):
                qt, g = divmod(i, 2)
                sb_ = i % 2
                act(PTB[:, sb_, :], PS(3 * sb_, 3), AF.Exp, psk(3 * sb_, 3), [("PTB", sb_)], scale=SCALE)
                if i + 1 < total:
                    qk3(i + 1)
                for blk in range(3):
                    kb = qt + blk
                    mm(PS(6), V_B[:, kb, g * 64:g * 64 + 128], PTB[:, sb_, blk * 512:(blk + 1) * 512],
                       blk == 0, False, [("PTB", sb_), ("V_B",)], psk(6))
                mm(PS(6), sel[0:1, (0 if g == 0 else 64):(128 if g == 0 else 192)],
                   esink[0:1, g * 4:(g + 1) * 4, :].rearrange("p j q -> p (j q)"), False, True,
                   [("sel",), ("esink",)], psk(6))
                par = i % 2
                norm_part1(qt, par, (6, 6), (g,))
                deferred[i + 1] = (lambda qt=qt, par=par, g=g: norm_part2(qt, par, (7, 7), (g,), ybT, ("ybT",)))
                if i in deferred:
                    deferred.pop(i)()
            for k in sorted(deferred):
                deferred[k]()
        A.free("PTB", "osb", "rrow", "KT_B", "V_B", "QT_B",
               "qn_bc", "kn_bc", "maskb", "sel", "esink")

        x1 = A.tile("x1", [NOWN, D], F32, at=X1_OFF)
        xs = A.tile("xs4", [2, D], F32)
        xn = A.tile("xn4", [1, D], BF16)
        hT = A.tile("hT4", [1, 8, 128], BF16)
        sg = A.tile("sg", [D], F32)
        m12 = A.tile("m12", [D], F32)
        mg = A.tile("mg", [D], BF16)
        mT = A.tile("mT", [8, 128], BF16)
        tmp = A.tile("tmp4", [D], F32)
        if stop_after >= 4:
            def front4(t):
                s2 = t % 2
                dma("sp", xs[:, s2], x_own[t * 128:(t + 1) * 128, :], f"x4_{s2}", (), [("xs4", s2)])
                act(xn[:, 0], xs[:, s2], AF.Copy, [("xs4", s2), ("rstd_own", t)], [("xn4",)],
                    scale=rstd_own[:, t:t + 1])
                pT = PS(6, 1, BF16)
                for kc in range(8):
                    tr(pT[:, kc * 128:(kc + 1) * 128], xn[:, 0, kc * 128:(kc + 1) * 128], ident_b,
                       [("xn4",), ("ident_b",)], psk(6))
                for kc in range(8):
                    o = hT[:, 0, kc, :]
                    i = pT[:, kc * 128:(kc + 1) * 128]
                    act(o, i, AF.Identity, psk(6) + [("G1c",), ("modT", "a")], [("hT4", kc)],
                        scale=G1c[:, kc:kc + 1], bias=S1c[:, kc:kc + 1])

            def mid4(t):
                s2 = t % 2
                for bi, (wt, nm, srcT, skey, dst, dkey) in enumerate((
                        (WbrA, "WbrA", yaT, ("yaT",), m12, ("m12",)),
                        (WbrB, "WbrB", ybT, ("ybT",), tmp, ("tmp4",)))):
                    for c in range(2):
                        c0 = bi * 1024 + c * 512
                        for kc in range(8):
                            mm(PS(c), hT[:, 0, kc, :], Wg[:, kc, c0:c0 + 512], kc == 0, kc == 7,
                               [("hT4", kc), ("Wg",)], psk(c))
                    act(sg, PS(0, 2), AF.Sigmoid, psk(0, 2), [("sg",)])
                    for c in range(2):
                        for j in range(4):
                            mm(PS(2 + 2 * bi + c), srcT[:, j, t * 128:(t + 1) * 128], wt[:, j, c * 512:(c + 1) * 512],
                               j == 0, j == 3, [skey, (nm, 0), (nm, 1)], psk(2 + 2 * bi + c))
                    tt("dve", dst, sg, PS(2 + 2 * bi, 2), ALU.mult, [("sg",)] + psk(2 + 2 * bi, 2), [dkey])

            def tail4(t):
                s2 = t % 2
                tt("pool", mg, m12, tmp, ALU.add, [("m12",), ("tmp4",)], [("mg",)])
                pT2 = PS(7, 1, BF16)
                for kc in range(8):
                    tr(pT2[:, kc * 128:(kc + 1) * 128], mg[:, kc * 128:(kc + 1) * 128], ident_b,
                       [("mg",), ("ident_b",)], psk(7))
                cp("dve", mT, pT2.rearrange("p (k q) -> p k q", k=8), psk(7), [("mT",)])
                for c in range(2):
                    for kc in range(8):
                        mm(PS(c), mT[:, kc, :], Wo[:, kc, c * 512:(c + 1) * 512], kc == 0, kc == 7,
                           [("mT",), ("Wo",)], psk(c))
                tt("dve", tmp, PS(0, 2), gate1_bc, ALU.mult, psk(0, 2) + [("gate_bc", 0)], [("tmp4",)])
                tt("pool", x1[:, t, :], tmp, xs[:, s2], ALU.add, [("tmp4",), ("xs4", s2)], [("x1", t)])

            front4(0)
            for t in range(NOWN):
                mid4(t)
                if t + 1 < NOWN:
                    front4(t + 1)
                tail4(t)
        A.free("Wg", "WbrA", "WbrB", "Wo", "xs4", "xn4", "hT4", "sg", "m12", "mg", "mT", "tmp4", "yaT", "ybT")

        h2T = A.tile("h2T", [8, NOWN * 128], BF16)
        W1q = A.tile("W1q", [2, 8, 1024], BF16)
        W2q = A.tile("W2q", [2, 8, 1024], BF16)
        hidT = A.tile("hidT", [2, 8, 512], BF16)
        rbuf = A.tile("rbuf", [2, 512], F32)
        xn = A.tile("xn5", [1, D], BF16)
        sqj = A.tile("sqj5", [D], BF16)
        tmp = A.tile("tmp5", [1, D], F32)
        obuf = A.tile("obuf", [1, D], F32)
        if stop_after >= 5:
            w1v = w1.rearrange("(kc p) n -> p kc n", p=128)
            w2v = w2.rearrange("(hc p) n -> p hc n", p=128)

            def load_q(qh):
                qs_ = qh % 2
                dma("pool", W1q[:, qs_], w1v[:, :, qh * 1024:(qh + 1) * 1024], f"w1_{qs_}", (), [("W1q", qs_)])
                dma("pool", W2q[:, qs_], w2v[:, qh * 8:(qh + 1) * 8, :], f"w2_{qs_}", (), [("W2q", qs_)])

            load_q(0)
            load_q(1)
            for t in range(NOWN):
                s2 = t % 2
                ssc = stat[:, 64 + s2:65 + s2]
                act(sqj, x1[:, t, :], AF.Square, [("x1", t)], [("sqj5",), ("stat", "ss5", s2)], accum_out=ssc)
                rsc = stat[:, 66 + s2:67 + s2]
                act(rsc, ssc, AF.Sqrt, [("stat", "ss5", s2)], [("stat", "rs5", s2)], scale=1.0 / D, bias=EPS)
                recip(rsc, rsc, [("stat", "rs5", s2)], [("stat", "rs5", s2)])
                act(xn[:, 0], x1[:, t, :], AF.Copy, [("x1", t), ("stat", "rs5", s2)], [("xn5",)], scale=rsc)
                for kc in range(8):
                    bk = 6 + kc // 4
                    tr(PS(bk, 1, BF16)[:, (kc % 4) * 128:(kc % 4 + 1) * 128], xn[:, 0, kc * 128:(kc + 1) * 128],
                       ident_b, [("xn5",), ("ident_b",)], psk(bk))
                for kc in range(8):
                    bk = 6 + kc // 4
                    o = h2T[:, kc, t * 128:(t + 1) * 128]
                    i = PS(bk, 1, BF16)[:, (kc % 4) * 128:(kc % 4 + 1) * 128]
                    if bk == 6:
                        act(o, i, AF.Identity, psk(bk) + [("G2c",), ("modT", "b")], [("h2T", t // 4, kc)],
                            scale=G2c[:, kc:kc + 1], bias=S2c[:, kc:kc + 1])
                    else:
                        ts("dve", o, i, G2c[:, kc:kc + 1], S2c[:, kc:kc + 1], ALU.mult, ALU.add,
                           psk(bk) + [("G2c",), ("modT", "b")], [("h2T", t // 4, kc)])

            hcount = [0]

            def hid(qh, grp):
                qs_ = qh % 2
                hs = (qh * 4 + grp) % 2
                for hc in range(8):
                    hb = hcount[0] % 4
                    rb = hcount[0] % 2
                    hcount[0] += 1
                    for kc in range(8):
                        mm(PS(hb), W1q[:, qs_, kc, hc * 128:(hc + 1) * 128], h2T[:, kc, grp * 512:(grp + 1) * 512],
                           kc == 0, kc == 7, [("W1q", qs_), ("h2T", grp, kc)], psk(hb))
                    act(rbuf[:, rb, :], PS(hb), AF.Relu, psk(hb), [("rbuf", rb)])
                    tt("pool", hidT[:, hs, hc, :], rbuf[:, rb, :], rbuf[:, rb, :], ALU.mult,
                       [("rbuf", rb)], [("hidT", hs, hc)])

            ycount = [0]

            def ymm(qh, grp):
                qs_ = qh % 2
                hs = (qh * 4 + grp) % 2
                for tt_ in range(4):
                    t = grp * 4 + tt_
                    yb = 4 + 2 * (ycount[0] % 2)
                    ys = ycount[0] % 2
                    ycount[0] += 1
                    for c in range(2):
                        for hc in range(8):
                            mm(PS(yb + c), hidT[:, hs, hc, tt_ * 128:(tt_ + 1) * 128],
                               W2q[:, qs_, hc, c * 512:(c + 1) * 512], hc == 0, hc == 7,
                               [("hidT", hs, hc), ("W2q", qs_)], psk(yb + c))
                    tt("dve", tmp[:, 0, :], PS(yb, 2), gate2_bc, ALU.mult, psk(yb, 2) + [("gate_bc", 1)],
                       [("tmp5",)])
                    tt("pool", x1[:, t, :], x1[:, t, :], tmp[:, 0, :], ALU.add, [("x1", t), ("tmp5",)],
                       [("x1", t)])
                    if qh == 3:
                        ssc = stat[:, 68 + ys:69 + ys]
                        act(sqj, x1[:, t, :], AF.Square, [("x1", t)], [("sqj5",), ("stat", "ss6", ys)],
                            accum_out=ssc)
                        rsc = stat[:, 70 + ys:71 + ys]
                        act(rsc, ssc, AF.Sqrt, [("stat", "ss6", ys)], [("stat", "rs6", ys)], scale=1.0 / D, bias=EPS)
                        recip(rsc, rsc, [("stat", "rs6", ys)], [("stat", "rs6", ys)])
                        stt("dve", obuf[:, 0, :], x1[:, t, :], rsc, gf_bc, ALU.mult, ALU.mult,
                            [("x1", t), ("stat", "rs6", ys), ("gf_bc",)], [("obuf",)])
                        dma("sp", out[t * 128:(t + 1) * 128, :], obuf[:, 0, :], "o_0", [("obuf",)],
                            [("out", t)])

            seq = [(qh, grp) for qh in range(4) for grp in range(4)]
            hid(*seq[0])
            for n_, (qh, grp) in enumerate(seq):
                if n_ + 1 < len(seq):
                    hid(*seq[n_ + 1])
                ymm(qh, grp)
                if grp == 3 and qh + 2 < 4:
                    load_q(qh + 2)
            S.add("sp", lambda e: e.nop(), [("out", t) for t in range(NOWN)], ())

        for (nm, keys) in dumps:
            fv, dt_ = A.flat[nm]
            d_ap = nc.dram_tensor("dbg_" + nm, [128, fv.shape[1]], F32, kind="ExternalOutput").ap()
            dma("pool", d_ap, fv, "dbg_" + nm, list(keys), [("dbgout", nm)])
            S.add("sp", lambda e: e.nop(), [("dbgout", nm)], ())
        S.emit()
        nc._arena_peak = A.peak
    return nc


def _rope_tables():
    t = np.arange(SEQ)
    row = (t // 64).astype(np.float32)
    col = (t % 64).astype(np.float32)
    inv16 = (10000.0 ** (-np.arange(0, 32, 2, dtype=np.float32) / 32)).astype(np.float32)
    inv32 = (10000.0 ** (-np.arange(0, 64, 2, dtype=np.float32) / 64)).astype(np.float32)
    A_ = np.zeros((SEQ, 128), np.float32)
    for b, pos in enumerate((row, col)):
        ang = pos[:, None] * inv16[None, :]
        cs, sn = np.cos(ang).astype(np.float32), np.sin(ang).astype(np.float32)
        A_[:, b * 32:b * 32 + 16] = cs
        A_[:, b * 32 + 16:b * 32 + 32] = cs
        A_[:, 64 + b * 32:64 + b * 32 + 16] = -sn
        A_[:, 64 + b * 32 + 16:64 + b * 32 + 32] = sn
    B_ = np.zeros((SEQ, 128), np.float32)
    ang = t.astype(np.float32)[:, None] * inv32[None, :]
    cs, sn = np.cos(ang).astype(np.float32), np.sin(ang).astype(np.float32)
    B_[:, 0:32] = cs
    B_[:, 32:64] = cs
    B_[:, 64:96] = -sn
    B_[:, 96:128] = sn
    return A_, B_


def _masks(core):
    kl = np.arange(128)[:, None]
    ql = np.arange(128)[None, :]
    prev = np.where(kl >= ql, 0.0, NEG).astype(np.float32)
    nxt = np.where(kl <= ql, 0.0, NEG).astype(np.float32)
    allneg = np.full((128, 128), NEG, np.float32)
    first = allneg if core % 4 == 0 else prev
    last = allneg if core % 4 == 3 else nxt
    return np.stack([np.tile(m, (1, 4)) for m in (first, prev, nxt, last)]).astype(np.float32)


def make_in_maps(x, c, w_ada, b_ada, norm1_g, w_in, q_norm_a, k_norm_a, sink_b,
                 w_branch, w_out, norm2_g, w_mlp_in, w_mlp_out, final_g):
    f = lambda a: np.ascontiguousarray(np.asarray(a, dtype=np.float32))
    x = f(x); c = f(c)
    ropeA, ropeB = _rope_tables()
    pk = lambda v: f(np.asarray(v, np.float32).reshape(-1, 128).T)
    shared = {
        "w_ada": f(w_ada[0]), "bada_pk": pk(b_ada[0]), "g1_pk": pk(norm1_g[0]), "g2_pk": pk(norm2_g[0]),
        "gf": f(final_g), "w_in": f(w_in[0]), "qn_g": f(q_norm_a[0]), "kn_g": f(k_norm_a[0]),
        "sink": f(np.asarray(sink_b[0]).reshape(1, 8)), "w_br": f(w_branch[0]), "w_out": f(w_out[0]),
        "w1": f(w_mlp_in[0]), "w2": f(w_mlp_out[0]),
    }
    in_maps = []
    for core in range(N_CORES):
        b, qi = divmod(core, 4)
        q0 = qi * 2048
        own = slice(q0, q0 + 2048)
        oth_idx = np.concatenate([np.arange(0, q0), np.arange(q0 + 2048, SEQ)])
        halo = np.zeros((256, D), np.float32)
        ropeB_h = np.zeros((256, 128), np.float32)
        if q0 > 0:
            halo[0:128] = x[b, q0 - 128:q0]
            ropeB_h[0:128] = ropeB[q0 - 128:q0]
        if q0 + 2048 < SEQ:
            halo[128:256] = x[b, q0 + 2048:q0 + 2176]
            ropeB_h[128:256] = ropeB[q0 + 2048:q0 + 2176]
        m = dict(shared)
        m.update({
            "x_own": f(x[b, own]), "x_oth": f(x[b, oth_idx]), "x_halo": halo,
            "c_pk": pk(c[b]),
            "ropeA_own": f(ropeA[own]), "ropeA_oth": f(ropeA[oth_idx]),
            "ropeB_own": f(ropeB[own]), "ropeB_halo": ropeB_h,
            "masks": _masks(core),
        })
        in_maps.append(m)
    return in_maps


_NC_CACHE = {}


def kernel(x, c, w_ada, b_ada, norm1_g, w_in, q_norm_a, k_norm_a, sink_b,
           w_branch, w_out, norm2_g, w_mlp_in, w_mlp_out, final_g):
    in_maps = make_in_maps(x, c, w_ada, b_ada, norm1_g, w_in, q_norm_a, k_norm_a, sink_b,
                           w_branch, w_out, norm2_g, w_mlp_in, w_mlp_out, final_g)
    if "nc" not in _NC_CACHE:
        _NC_CACHE["nc"] = build()
    nc = _NC_CACHE["nc"]
    res = run_bass_kernel_spmd(nc, in_maps, core_ids=list(range(N_CORES)))
    outp = np.zeros((2, SEQ, D), np.float32)
    for core in range(N_CORES):
        b, qi = divmod(core, 4)
        outp[b, qi * 2048:(qi + 1) * 2048] = np.asarray(res.results[core]["out"], dtype=np.float32)
    return outp
```
